# Optimizing a Trainium2 kernel written in Bass

```python
import math
import jax, jax.numpy as jnp
from jax import lax
import numpy as np

D_MODEL = 1024
BATCH = 4
SEQ = 8192
DEPTH = 1

GRID_W = 64
CTX_LEN = 256
EPS = 1e-6

SSD_HEADS = 16
SSD_HEAD_DIM = 64
SSD_INNER = SSD_HEADS * SSD_HEAD_DIM
SSD_GROUPS = 2
SSD_HPG = SSD_HEADS // SSD_GROUPS
SSD_STATE = 128
SSD_CONV = 3
SSD_CHUNK = 128
BC_WIDTH = SSD_GROUPS * SSD_STATE
XBC_WIDTH = SSD_INNER + 2 * BC_WIDTH

DA_HEADS = 4
DA_QK_DIM = 64
DA_V_DIM = 2 * DA_QK_DIM
DA_INNER = DA_HEADS * DA_V_DIM
Q_BLOCK = 128
ROPE_BASE = 10000.0
ROPE_AXIS_DIM = DA_QK_DIM // 2

MIX_WIDTH = SSD_INNER + DA_INNER
SPLITS = (SSD_INNER,
          SSD_INNER + XBC_WIDTH,
          SSD_INNER + XBC_WIDTH + 2 * SSD_HEADS,
          SSD_INNER + XBC_WIDTH + 2 * SSD_HEADS + DA_INNER,
          SSD_INNER + XBC_WIDTH + 2 * SSD_HEADS + 2 * DA_INNER)
IN_WIDTH = SPLITS[-1] + DA_INNER

D_FF = 2816
FFN_CONV = 3

kernel_name = 'hybrid_ssd_diffattn_dit_block'

F32 = jnp.float32


def rmsnorm(x, g):
    xf = x.astype(F32)
    y = xf * lax.rsqrt(jnp.mean(xf * xf, axis=-1, keepdims=True) + EPS)
    return (y * g.astype(F32)).astype(x.dtype)


def modulate(h, shift, scale):
    return h * (1 + scale) + shift


def dwconv_centred(x, w, b):
    k = w.shape[0]
    pad = k // 2
    n = x.shape[1]
    xp = jnp.pad(x, ((0, 0), (pad, pad), (0, 0)))
    out = b
    for j in range(k):
        out = out + xp[:, j:j + n] * w[j]
    return out


def axial_rope(row, col):
    inv = jnp.power(ROPE_BASE, -jnp.arange(0, ROPE_AXIS_DIM, 2, dtype=F32) / ROPE_AXIS_DIM)
    ar = row.astype(F32)[:, None] * inv
    ac = col.astype(F32)[:, None] * inv
    ang = jnp.concatenate([ar, ar, ac, ac], axis=-1)
    return jnp.cos(ang), jnp.sin(ang)


def _rot_half(u):
    u1, u2 = jnp.split(u, 2, axis=-1)
    return jnp.concatenate([-u2, u1], axis=-1)


def apply_rope(t, cos, sin):
    cos = cos[None, :, None, None, :].astype(t.dtype)
    sin = sin[None, :, None, None, :].astype(t.dtype)
    r = jnp.concatenate([_rot_half(t[..., :ROPE_AXIS_DIM]), _rot_half(t[..., ROPE_AXIS_DIM:])], axis=-1)
    return t * cos + r * sin


def diff_attention(q, k, v, lam):
    s = jnp.einsum('bqhcd,bkhcd->bhcqk', q.astype(F32), k.astype(F32)) * (DA_QK_DIM ** -0.5)
    p = jax.nn.softmax(s, axis=-1)
    w = p[:, :, 0] - lam * p[:, :, 1]
    return jnp.einsum('bhqk,bkhe->bqhe', w, v.astype(F32))


def diff_attention_blocks(q, k, v, lam):
    b, n = q.shape[:2]
    nb = n // Q_BLOCK
    qb = q.reshape(b, nb, Q_BLOCK, DA_HEADS, 2, DA_QK_DIM).swapaxes(0, 1)
    ob = lax.map(lambda qi: diff_attention(qi, k, v, lam), qb)
    return ob.swapaxes(0, 1).reshape(b, n, DA_HEADS, DA_V_DIM)


def _chunk(t):
    return t.reshape(t.shape[0], t.shape[1] // SSD_CHUNK, SSD_CHUNK, *t.shape[2:])


def ssd_inputs(xbc, dt_raw, dt_bias_d, reverse):
    b, n = xbc.shape[:2]
    xf = xbc.astype(F32)
    xs = xf[..., :SSD_INNER].reshape(b, n, SSD_GROUPS, SSD_HPG, SSD_HEAD_DIM)
    bm = xf[..., SSD_INNER:SSD_INNER + BC_WIDTH].reshape(b, n, SSD_GROUPS, SSD_STATE)
    cm = xf[..., SSD_INNER + BC_WIDTH:].reshape(b, n, SSD_GROUPS, SSD_STATE)
    dt = jax.nn.softplus(dt_raw.astype(F32) + dt_bias_d.astype(F32)).reshape(b, n, SSD_GROUPS, SSD_HPG)
    parts = (xs, dt, bm, cm)
    if reverse:
        parts = tuple(jnp.flip(t, axis=1) for t in parts)
    return tuple(_chunk(t) for t in parts)


def ssd_chunk_states(xs, dt, bm, a_cum, h0):
    decay_to_end = jnp.exp(a_cum[:, :, -1:] - a_cum)
    chunk_states = jnp.einsum('bclgn,bclgr,bclgrp->bcgrpn', bm, decay_to_end * dt, xs)
    chunk_decay = jnp.exp(a_cum[:, :, -1])

    def step(h, inp):
        s, d = inp
        return d[..., None, None] * h + s, h

    h_final, h_starts = lax.scan(step, h0, (chunk_states.swapaxes(0, 1), chunk_decay.swapaxes(0, 1)))
    return h_starts.swapaxes(0, 1), h_final


def ssd_chunk_outputs(xs, dt, bm, cm, a_cum, h_starts):
    lc = a_cum.shape[2]
    causal = jnp.tril(jnp.ones((lc, lc), dtype=bool))[None, None, :, :, None, None]
    seg = a_cum[:, :, :, None] - a_cum[:, :, None, :]
    decay = jnp.exp(jnp.where(causal, seg, -jnp.inf))
    scores = jnp.einsum('bclgn,bcsgn->bclsg', cm, bm)
    w = scores[..., None] * decay * dt[:, :, None]
    y_diag = jnp.einsum('bclsgr,bcsgrp->bclgrp', w, xs)
    y_off = jnp.einsum('bclgn,bcgrpn,bclgr->bclgrp', cm, h_starts, jnp.exp(a_cum))
    return y_diag + y_off


def _ssd_finish(y, reverse):
    b = y.shape[0]
    y = y.reshape(b, -1, SSD_INNER)
    return jnp.flip(y, axis=1) if reverse else y


def ssd_bidir(xbc_l, dt_l, xbc_c, dt_c, a_log, dt_bias, d_skip, need_ctx):
    b = xbc_l.shape[0]
    y_l, y_c = None, None
    for d in range(2):
        rev = d == 1
        a = -jnp.exp(a_log[d].astype(F32)).reshape(SSD_GROUPS, SSD_HPG)
        dsk = d_skip[d].astype(F32).reshape(SSD_GROUPS, SSD_HPG, 1)
        cols = slice(d * SSD_HEADS, (d + 1) * SSD_HEADS)
        xs_c, dtc, b_c, c_c = ssd_inputs(xbc_c, dt_c[..., cols], dt_bias[d], rev)
        acum_c = jnp.cumsum(dtc * a, axis=2)
        h0 = jnp.zeros((b, SSD_GROUPS, SSD_HPG, SSD_HEAD_DIM, SSD_STATE), F32)
        hs_c, h_ctx = ssd_chunk_states(xs_c, dtc, b_c, acum_c, h0)
        xs_l, dtl, b_l, c_l = ssd_inputs(xbc_l, dt_l[..., cols], dt_bias[d], rev)
        acum_l = jnp.cumsum(dtl * a, axis=2)
        hs_l, _ = ssd_chunk_states(xs_l, dtl, b_l, acum_l, h_ctx)
        yd = _ssd_finish(ssd_chunk_outputs(xs_l, dtl, b_l, c_l, acum_l, hs_l) + dsk * xs_l, rev)
        y_l = yd if y_l is None else y_l + yd
        if need_ctx:
            ydc = _ssd_finish(ssd_chunk_outputs(xs_c, dtc, b_c, c_c, acum_c, hs_c) + dsk * xs_c, rev)
            y_c = ydc if y_c is None else y_c + ydc
    return y_l, y_c


def token_mixer(pl, pc, cos, sin, conv_w, conv_b, a_log, dt_bias, d_skip, ssd_norm_g,
                lam, lam_init, subln_g, need_ctx):
    b, n = pl.shape[:2]
    nc = pc.shape[1]
    zl, xbcl, dtl, ql, kl, vl = jnp.split(pl, SPLITS, axis=-1)
    zc, xbcc, dtc, qc, kc, vc = jnp.split(pc, SPLITS, axis=-1)
    xbcl = jax.nn.silu(dwconv_centred(xbcl, conv_w, conv_b))
    xbcc = jax.nn.silu(dwconv_centred(xbcc, conv_w, conv_b))
    ys_l, ys_c = ssd_bidir(xbcl, dtl, xbcc, dtc, a_log, dt_bias, d_skip, need_ctx)
    ys_l = rmsnorm(ys_l.astype(pl.dtype) * jax.nn.silu(zl), ssd_norm_g)
    ql = apply_rope(ql.reshape(b, n, DA_HEADS, 2, DA_QK_DIM), cos, sin)
    kl = apply_rope(kl.reshape(b, n, DA_HEADS, 2, DA_QK_DIM), cos, sin)
    kc = kc.reshape(b, nc, DA_HEADS, 2, DA_QK_DIM)
    vc = vc.reshape(b, nc, DA_HEADS, DA_V_DIM)
    k_all = jnp.concatenate([kc, kl], axis=1)
    v_all = jnp.concatenate([vc, vl.reshape(b, n, DA_HEADS, DA_V_DIM)], axis=1)
    o_l = diff_attention_blocks(ql, k_all, v_all, lam).astype(pl.dtype)
    o_l = (rmsnorm(o_l, subln_g) * (1 - lam_init)).reshape(b, n, DA_INNER)
    yl = jnp.concatenate([ys_l, o_l], axis=-1)
    yc = None
    if need_ctx:
        o_c = diff_attention(qc.reshape(b, nc, DA_HEADS, 2, DA_QK_DIM), kc, vc, lam).astype(pc.dtype)
        o_c = (rmsnorm(o_c, subln_g) * (1 - lam_init)).reshape(b, nc, DA_INNER)
        ys_c = rmsnorm(ys_c.astype(pc.dtype) * jax.nn.silu(zc), ssd_norm_g)
        yc = jnp.concatenate([ys_c, o_c], axis=-1)
    return yl, yc


def conv_ffn(h, w_up, cw, cb, w_down):
    val, gate = jnp.split(h @ w_up, 2, axis=-1)
    gate = dwconv_centred(gate, cw, cb)
    return (jax.nn.silu(gate) * val) @ w_down


def setup_inputs(seed: int = 0) -> dict:
    key = jax.random.key(seed)
    ks = jax.random.split(key, 26)
    L = DEPTH

    def nrm(k, shape, scale):
        return jax.random.normal(k, shape, F32) * scale

    dt0 = jnp.exp(jax.random.uniform(ks[10], (L, 2, SSD_HEADS), F32, math.log(1e-3), math.log(1e-1)))
    return {
        'x': nrm(ks[0], (BATCH, SEQ, D_MODEL), 1.0),
        'c': nrm(ks[1], (BATCH, D_MODEL), 1.0),
        'ctx': nrm(ks[2], (BATCH, CTX_LEN, D_MODEL), 1.0),
        'c_ctx': nrm(ks[3], (D_MODEL,), 1.0),
        'w_mod': nrm(ks[4], (L, D_MODEL, 6 * D_MODEL), 0.5 * D_MODEL ** -0.5),
        'b_mod': nrm(ks[5], (L, 6 * D_MODEL), 0.01),
        'norm1_g': 1.0 + nrm(ks[6], (L, D_MODEL), 0.02),
        'w_in': nrm(ks[7], (L, D_MODEL, IN_WIDTH), D_MODEL ** -0.5),
        'conv_w': nrm(ks[8], (L, SSD_CONV, XBC_WIDTH), SSD_CONV ** -0.5),
        'conv_b': nrm(ks[9], (L, XBC_WIDTH), 0.02),
        'a_log': jnp.log(jax.random.uniform(ks[11], (L, 2, SSD_HEADS), F32, 1.0, 16.0)),
        'dt_bias': dt0 + jnp.log(-jnp.expm1(-dt0)),
        'd_skip': 1.0 + nrm(ks[12], (L, 2, SSD_HEADS), 0.1),
        'ssd_norm_g': 1.0 + nrm(ks[13], (L, SSD_INNER), 0.02),
        'lam_q1': nrm(ks[14], (L, DA_QK_DIM), 0.1),
        'lam_k1': nrm(ks[15], (L, DA_QK_DIM), 0.1),
        'lam_q2': nrm(ks[16], (L, DA_QK_DIM), 0.1),
        'lam_k2': nrm(ks[17], (L, DA_QK_DIM), 0.1),
        'subln_g': 1.0 + nrm(ks[18], (L, DA_V_DIM), 0.02),
        'w_out': nrm(ks[19], (L, MIX_WIDTH, D_MODEL), MIX_WIDTH ** -0.5),
        'norm2_g': 1.0 + nrm(ks[20], (L, D_MODEL), 0.02),
        'w_up': nrm(ks[21], (L, D_MODEL, 2 * D_FF), D_MODEL ** -0.5),
        'ffn_conv_w': nrm(ks[22], (L, FFN_CONV, D_FF), FFN_CONV ** -0.5),
        'ffn_conv_b': nrm(ks[23], (L, D_FF), 0.02),
        'w_down': nrm(ks[24], (L, D_FF, D_MODEL), D_FF ** -0.5),
        'final_g': 1.0 + nrm(ks[25], (D_MODEL,), 0.02),
    }


def reference(x, c, ctx, c_ctx, w_mod, b_mod, norm1_g, w_in, conv_w, conv_b, a_log, dt_bias,
              d_skip, ssd_norm_g, lam_q1, lam_k1, lam_q2, lam_k2, subln_g, w_out, norm2_g,
              w_up, ffn_conv_w, ffn_conv_b, w_down, final_g):
    n_lat = x.shape[1]
    rows = n_lat // GRID_W
    row = jnp.repeat(jnp.arange(rows), GRID_W)
    col = jnp.tile(jnp.arange(GRID_W), rows)
    cos, sin = axial_rope(row, col)
    xl, xc = x, ctx
    for i in range(DEPTH):
        need_ctx = i < DEPTH - 1
        lam_init = 0.8 - 0.6 * math.exp(-0.3 * i)
        lam = (jnp.exp(jnp.sum(lam_q1[i].astype(F32) * lam_k1[i].astype(F32)))
               - jnp.exp(jnp.sum(lam_q2[i].astype(F32) * lam_k2[i].astype(F32))) + lam_init)
        mod_l = jnp.split((jax.nn.silu(c) @ w_mod[i] + b_mod[i])[:, None, :], 6, axis=-1)
        mod_c = jnp.split((jax.nn.silu(c_ctx) @ w_mod[i] + b_mod[i])[None, None, :], 6, axis=-1)
        hl = modulate(rmsnorm(xl, norm1_g[i]), mod_l[0], mod_l[1])
        hc = modulate(rmsnorm(xc, norm1_g[i]), mod_c[0], mod_c[1])
        yl, yc = token_mixer(hl @ w_in[i], hc @ w_in[i], cos, sin, conv_w[i], conv_b[i], a_log[i],
                             dt_bias[i], d_skip[i], ssd_norm_g[i], lam, lam_init, subln_g[i], need_ctx)
        xl = xl + mod_l[2] * (yl @ w_out[i])
        xl = xl + mod_l[5] * conv_ffn(modulate(rmsnorm(xl, norm2_g[i]), mod_l[3], mod_l[4]),
                                      w_up[i], ffn_conv_w[i], ffn_conv_b[i], w_down[i])
        if need_ctx:
            xc = xc + mod_c[2] * (yc @ w_out[i])
            xc = xc + mod_c[5] * conv_ffn(modulate(rmsnorm(xc, norm2_g[i]), mod_c[3], mod_c[4]),
                                          w_up[i], ffn_conv_w[i], ffn_conv_b[i], w_down[i])
    return rmsnorm(xl, final_g)
```

```python
import math
from contextlib import ExitStack

import numpy as np
import concourse.bass as bass
import concourse.mybir as mybir
from concourse.bass_utils import run_bass_kernel_spmd

F32 = mybir.dt.float32
BF16 = mybir.dt.bfloat16
AF = mybir.ActivationFunctionType
ALU = mybir.AluOpType

D = 1024
CTX = 256
INW = 4128
XBC0, DT0, Q0, K0, V0 = 1024, 2560, 2592, 3104, 3616
DFF = 2816
EPS = 1e-6


class Buf:
    __slots__ = ("name", "w", "r")

    def __init__(self, name=""):
        self.name = name
        self.w = None
        self.r = []


class Op:
    __slots__ = ("eng", "fn", "dma", "deps", "sig", "sem", "val", "waits", "ph")


ENGS = ("pe", "act", "dve", "pool", "sp")
SEM_CAP = 30000
DMA_K = 8


class Prog:
    def __init__(self, nc, stack, same_eng_sync=("pool", "dve", "act")):
        self.nc = nc
        self.stack = stack
        self.same = same_eng_sync
        self.ops = []
        self.ph = 0
        self.nsig = {e: 0 for e in ENGS}
        self.ndma = {e: 0 for e in ENGS}
        self.sems = {e: [] for e in ENGS}
        self.dsems = {e: [stack.enter_context(nc.semaphore(f"d_{e}_{i}")) for i in range(DMA_K)]
                      for e in ("sp", "pool", "act")}
        self.waited = {e: {} for e in ENGS}
        self.bar = stack.enter_context(nc.semaphore("bar"))
        self.dfin = {e: {} for e in ENGS}
        self.count = {e: 0 for e in ENGS}

    def add(self, eng, fn, reads=(), writes=(), dma=False):
        op = Op()
        op.eng, op.fn, op.dma, op.sig, op.sem, op.val, op.waits, op.ph = eng, fn, dma, False, None, 0, [], self.ph
        deps = set()
        for b in reads:
            if b.w is not None:
                deps.add(b.w)
        for b in writes:
            if b.w is not None:
                deps.add(b.w)
            deps.update(b.r)
        for b in reads:
            if (not dma) and eng in ("pe", "act", "dve"):
                b.r = [r for r in b.r if r.dma or r.eng != eng]
            b.r.append(op)
        for b in writes:
            b.w = op
            b.r = []
        deps.discard(op)
        op.deps = [d for d in deps if d.ph == self.ph]
        self.ops.append(op)
        return op

    def pe(self, fn, r=(), w=()):
        return self.add("pe", fn, r, w)

    def act(self, fn, r=(), w=()):
        return self.add("act", fn, r, w)

    def dve(self, fn, r=(), w=()):
        return self.add("dve", fn, r, w)

    def pool(self, fn, r=(), w=()):
        return self.add("pool", fn, r, w)

    def dma(self, out, in_, r=(), w=(), q="sp", slow=False):
        if slow:
            return self.add(q, lambda e: e.dma_start(out=out, in_=in_, allow_slow_non_contiguous=True), r, w, dma=True)
        return self.add(q, lambda e: e.dma_start(out=out, in_=in_), r, w, dma=True)

    def _sem(self, e, m):
        i = m // SEM_CAP
        while len(self.sems[e]) <= i:
            self.sems[e].append(self.stack.enter_context(self.nc.semaphore(f"s_{e}_{len(self.sems[e])}")))
        return self.sems[e][i], (m % SEM_CAP) + 1

    def flush(self, final=False):
        ops = self.ops
        self.ops = []
        per = {e: [o for o in ops if o.eng == e] for e in ENGS}
        for op in ops:
            for d in op.deps:
                if d.dma:
                    continue
                if d.eng != op.eng or op.dma or (op.eng in self.same):
                    d.sig = True
        last = {}
        for e in ENGS:
            comp = [o for o in per[e] if not o.dma]
            if comp:
                comp[-1].sig = True
                last[e] = comp[-1]
        for e in ENGS:
            for o in per[e]:
                if o.dma:
                    j = self.ndma[e]
                    self.ndma[e] += 1
                    o.sem = self.dsems[e][j % DMA_K]
                    o.val = 16 * (j // DMA_K + 1)
                    self.dfin[e][o.sem] = o.val
                elif o.sig:
                    o.sem, o.val = self._sem(e, self.nsig[e])
                    self.nsig[e] += 1
        for e in ENGS:
            waited = self.waited[e]
            for o in per[e]:
                need = {}
                if o.dma and o.val > 16:
                    need[o.sem] = o.val - 16
                for d in o.deps:
                    if d.sem is None:
                        continue
                    if (not d.dma) and d.eng == e and not (o.dma or (e in self.same)):
                        continue
                    if need.get(d.sem, 0) < d.val:
                        need[d.sem] = d.val
                for s, v in need.items():
                    if waited.get(s, 0) >= v:
                        continue
                    waited[s] = v
                    o.waits.append((s, v))
            self.count[e] += len(per[e])
        self.ph += 1
        target = 5 * self.ph
        bar = self.bar

        def run(eng, e):
            for o in per[e]:
                for s, v in o.waits:
                    eng.wait_ge(s, v)
                ins = o.fn(eng)
                if o.dma:
                    ins.then_inc(o.sem, 16)
                elif o.sig:
                    ins.then_inc(o.sem, 1)
            if e in last:
                eng.wait_ge(last[e].sem, last[e].val)
            for s, v in self.dfin[e].items():
                eng.wait_ge(s, v)
            eng.sem_inc(bar, 1)
            eng.wait_ge(bar, target)

        with self.nc.Block() as block:
            @block.tensor
            def _(eng):
                run(eng, "pe")

            @block.scalar
            def _(eng):
                run(eng, "act")

            @block.vector
            def _(eng):
                run(eng, "dve")

            @block.gpsimd
            def _(eng):
                run(eng, "pool")

            @block.sync
            def _(eng):
                run(eng, "sp")


def bc(ap, shape, axis):
    return ap.unsqueeze(axis).to_broadcast(shape)


def build(NL, debug=False, phases=5):
    NOWN = NL // 2
    NMIX = NOWN + 128
    T = CTX + NL
    NTL, NTM, NTT = NL // 128, NMIX // 128, T // 128
    nc = bass.Bass("TRN2", target_bir_lowering=False)

    def din(name, shape, dt=F32):
        return nc.dram_tensor(name, shape, dt, kind="ExternalInput").ap()

    xall = din("xall", [T, D])
    c_pp = din("c_pp", [128, 16])
    w_mod = din("w_mod", [D, 6 * D])
    b_mod = din("b_mod", [1, 6 * D])
    vec_pp = din("vec_pp", [128, 32])
    w_in = din("w_in", [D, INW])
    cw_pp = din("cw_pp", [128, 36])
    cb_pp = din("cb_pp", [128, 12])
    cb_row = din("cb_row", [1, 1536])
    ssd_row = din("ssd_row", [1, 96])
    lam_row = din("lam_row", [1, 256])
    w_out = din("w_out", [1536, D])
    w_up = din("w_up", [D, 2 * DFF])
    fw_pp = din("fw_pp", [128, 66])
    fb_pp = din("fb_pp", [128, 22])
    w_down = din("w_down", [DFF, D])
    fg_row = din("fg_row", [1, D])
    rope = din("rope", [NL, 128])
    out = nc.dram_tensor("out", [NOWN, D], F32, kind="ExternalOutput").ap()

    skind = "ExternalOutput" if debug else "Internal"

    def dscr(name, shape, dt):
        return nc.dram_tensor(name, shape, dt, kind=skind).ap()

    XS = dscr("XS", [T, D], BF16)
    BS = dscr("BS", [T, 256], BF16)
    BT = dscr("BT", [256, T], BF16)
    CT = dscr("CT", [256, NMIX], BF16)
    ZS = dscr("ZS", [NMIX, D], BF16)
    QT = dscr("QT", [4, 128, NMIX], BF16)
    KT = dscr("KT", [4, 128, T], BF16)
    VS = dscr("VS", [T, 512], BF16)
    YT2 = dscr("YT2", [NTM, 128, 12, 128], BF16)
    XL1 = dscr("XL1", [NMIX, D], F32)
    H2T = dscr("H2T", [D, NMIX + 2], BF16)
    HSd = dscr("HSd", [NTM, 128, D], BF16)
    DBG = dscr("DBG", [128, 4096], F32)

    with ExitStack() as top:
        P = Prog(nc, top)

        def sbt(st, name, shape, dt):
            return st.enter_context(nc.sbuf_tensor(name, shape, dt))

        def pst(st, name, shape, dt):
            return st.enter_context(nc.psum_tensor(name, shape, dt))

        ident = sbt(top, "ident", [128, 128], BF16)
        identf = sbt(top, "identf", [128, 128], F32)
        onesf = sbt(top, "onesf", [128, 128], F32)
        onesb = sbt(top, "onesb", [128, 128], BF16)
        TRI2 = sbt(top, "tri2", [128, 2, 128], F32)
        UU = sbt(top, "uu", [128, 2, 128], BF16)
        VEC = sbt(top, "vec", [128, 32], F32)
        MOD5B = sbt(top, "mod5b", [128, D], F32)
        PPM = sbt(top, "ppm", [128, 6, 8], F32)
        GSH = sbt(top, "gsh", [128, 6, 8], F32)
        SSDB = sbt(top, "ssdb", [128, 96], F32)
        LAMB = sbt(top, "lamb", [128, 256], F32)
        NLAM = sbt(top, "nlam", [128, 4], F32)
        FGB = sbt(top, "fgb", [128, D], F32)
        SGS = sbt(top, "sgs", [128, 1], F32)
        mid1 = top.enter_context(ExitStack())
        MOD2B = sbt(mid1, "mod2b", [128, D], F32)
        mid2 = top.enter_context(ExitStack())
        DTS = sbt(mid2, "dts", [128, NTT, 32], F32)
        DAS = sbt(mid2, "das", [128, NTT, 32], F32)
        bK = Buf("const")

        with ExitStack() as st:
            WM = sbt(st, "wm", [128, 8, 6 * D], BF16)
            BMr = sbt(st, "bmr", [1, 6 * D], BF16)
            CP = sbt(st, "cp", [128, 16], F32)
            SC = sbt(st, "sc", [128, 16], F32)
            SCB = sbt(st, "scb", [128, 16, 128], BF16)
            MODL = sbt(st, "modl", [128, 6 * D], F32)
            MODC = sbt(st, "modc", [128, 2 * D], F32)
            tmp = sbt(st, "tmp0", [128, 128], F32)
            pm = [pst(st, f"pm{i}", [128, 512], F32) for i in range(2)]
            ptr = pst(st, "ptr", [128, 4, 128], F32)
            bWM, bBM, bSC, bMOD, bPT = Buf(), Buf(), Buf(), Buf(), Buf()
            bpm = [Buf(), Buf()]
            for kc in range(8):
                P.dma(WM[:, kc, :], w_mod[kc * 128:(kc + 1) * 128, :], w=[bWM], q="pool")
            P.dma(BMr[:], b_mod, w=[bBM], q="pool")
            P.dma(CP[:], c_pp, w=[bSC])
            P.dma(VEC[:], vec_pp, w=[bK])
            P.dma(SSDB[:], ssd_row.partition_broadcast(128), w=[bK])
            P.dma(LAMB[:], lam_row.partition_broadcast(128), w=[bK])
            P.dma(FGB[:], fg_row.partition_broadcast(128), w=[bK])
            P.pool(lambda e: e.memset(identf[:], 1.0), w=[bK])
            P.pool(lambda e: e.memset(onesf[:], 1.0), w=[bK])
            P.pool(lambda e: e.memset(TRI2[:], 1.0), w=[bK])
            P.pool(lambda e: e.memset(tmp[:], 1.0), w=[bK])

            def asel(o, i, op, sgn=1):
                return lambda e: e.affine_select(out=o, in_=i, pattern=[[-sgn, 128]], compare_op=op, fill=0.0,
                                                 base=0, channel_multiplier=sgn)
            P.pool(asel(identf[:], identf[:], ALU.is_equal), r=[bK], w=[bK])
            P.pool(asel(TRI2[:, 0, :], TRI2[:, 0, :], ALU.is_ge, -1), r=[bK], w=[bK])
            P.pool(asel(TRI2[:, 1, :], TRI2[:, 1, :], ALU.is_ge, 1), r=[bK], w=[bK])
            P.dve(lambda e: e.tensor_copy(out=ident[:], in_=identf[:]), r=[bK], w=[bK])
            P.dve(lambda e: e.tensor_copy(out=onesb[:], in_=onesf[:]), r=[bK], w=[bK])
            P.dve(lambda e: e.tensor_scalar(out=UU[:], in0=TRI2[:], scalar1=-1.0, scalar2=1.0, op0=ALU.mult, op1=ALU.add),
                  r=[bK], w=[bK])
            P.dve(lambda e: e.tensor_tensor(out=LAMB[:, 0:64], in0=LAMB[:, 0:64], in1=LAMB[:, 64:128], op=ALU.mult), r=[bK], w=[bK])
            P.dve(lambda e: e.tensor_tensor(out=LAMB[:, 128:192], in0=LAMB[:, 128:192], in1=LAMB[:, 192:256], op=ALU.mult), r=[bK], w=[bK])
            P.dve(lambda e: e.tensor_reduce(out=NLAM[:, 0:1], in_=LAMB[:, 0:64], axis=mybir.AxisListType.X, op=ALU.add), r=[bK], w=[bK])
            P.dve(lambda e: e.tensor_reduce(out=NLAM[:, 1:2], in_=LAMB[:, 128:192], axis=mybir.AxisListType.X, op=ALU.add), r=[bK], w=[bK])
            P.act(lambda e: e.activation(out=NLAM[:, 2:4], in_=NLAM[:, 0:2], func=AF.Exp), r=[bK], w=[bK])
            P.dve(lambda e: e.scalar_tensor_tensor(out=NLAM[:, 0:1], in0=NLAM[:, 3:4], scalar=-0.2, in1=NLAM[:, 2:3],
                                                   op0=ALU.add, op1=ALU.subtract), r=[bK], w=[bK])
            P.act(lambda e: e.activation(out=SSDB[:, 0:32], in_=SSDB[:, 0:32], func=AF.Exp), r=[bK], w=[bK])
            P.dve(lambda e: e.tensor_scalar(out=SSDB[:, 0:32], in0=SSDB[:, 0:32], scalar1=-1.0, scalar2=None, op0=ALU.mult), r=[bK], w=[bK])
            P.dve(lambda e: e.tensor_tensor(out=SSDB[:, 64:80], in0=SSDB[:, 64:80], in1=SSDB[:, 80:96], op=ALU.add), r=[bK], w=[bK])
            P.dve(lambda e: e.tensor_scalar(out=SGS[:], in0=VEC[:, 24:25], scalar1=0.8, scalar2=None, op0=ALU.mult), r=[bK], w=[bK])
            P.act(lambda e: e.activation(out=SC[:], in_=CP[:], func=AF.Silu), r=[bSC], w=[bSC])
            P.dve(lambda e: e.tensor_copy(out=SCB[:], in_=bc(SC[:], [128, 16, 128], 2)), r=[bSC], w=[bSC])
            cnt = 0
            for v, ncg, dst in ((0, 12, MODL), (1, 4, MODC)):
                for cg in range(ncg):
                    pp, bp = pm[cnt % 2], bpm[cnt % 2]
                    cnt += 1
                    for kc in range(8):
                        P.pe(lambda e, pp=pp, v=v, kc=kc, cg=cg: e.matmul(pp[:], lhsT=SCB[:, v * 8 + kc, :], rhs=WM[:, kc, cg * 512:(cg + 1) * 512],
                                                                          start=(kc == 0), stop=False), r=[bSC, bWM], w=[bp])
                    P.pe(lambda e, pp=pp, cg=cg: e.matmul(pp[:], lhsT=onesb[0:1, :], rhs=BMr[0:1, cg * 512:(cg + 1) * 512], start=False, stop=True),
                         r=[bK, bBM], w=[bp])
                    P.dve(lambda e, pp=pp, dst=dst, cg=cg: e.tensor_copy(out=dst[:, cg * 512:(cg + 1) * 512], in_=pp[:]), r=[bp], w=[bMOD])
            P.dve(lambda e: e.tensor_copy(out=MOD2B[:], in_=MODL[:, 2 * D:3 * D]), r=[bMOD], w=[bK])
            P.dve(lambda e: e.tensor_copy(out=MOD5B[:], in_=MODL[:, 5 * D:6 * D]), r=[bMOD], w=[bK])
            for i, (src, m) in enumerate(((MODL, 0), (MODL, 1), (MODL, 3), (MODL, 4), (MODC, 0), (MODC, 1))):
                for half in range(2):
                    for k4 in range(4):
                        kc = half * 4 + k4
                        P.pe(lambda e, src=src, m=m, kc=kc, k4=k4: e.transpose(ptr[:, k4, :], src[:, m * D + kc * 128:m * D + (kc + 1) * 128], identf[:]),
                             r=[bMOD, bK], w=[bPT])
                    P.dve(lambda e, i=i, half=half: e.tensor_copy(out=PPM[:, i, half * 4:(half + 1) * 4], in_=ptr[:, :, 0]), r=[bPT], w=[bK])
            for o, msc, msh, g0 in ((0, 1, 0, 0), (2, 5, 4, 0), (4, 3, 2, 8)):
                P.dve(lambda e, o=o, msc=msc, g0=g0: e.scalar_tensor_tensor(out=GSH[:, o, :], in0=PPM[:, msc, :], scalar=1.0, in1=VEC[:, g0:g0 + 8],
                                                                           op0=ALU.add, op1=ALU.mult), r=[bK], w=[bK])
                P.dve(lambda e, o=o, msh=msh: e.tensor_copy(out=GSH[:, o + 1, :], in_=PPM[:, msh, :]), r=[bK], w=[bK])
            if debug:
                P.dma(DBG[:, 0:48], GSH[:].rearrange("p a b -> p (a b)"), r=[bK])
                P.dma(DBG[:, 48:52], NLAM[:], r=[bK])
                P.dma(DBG[:, 64:160], SSDB[:], r=[bK])
                P.dma(DBG[:, 1024:2048], MOD2B[:], r=[bK])
            P.flush()
        if phases <= 0:
            return nc

        with ExitStack() as st:
            WI = sbt(st, "wi", [128, 8, INW], BF16)
            CW = sbt(st, "cw", [128, 36], F32)
            CB = sbt(st, "cb", [128, 12], F32)
            CO10 = sbt(st, "co10", [128, 10, 512], BF16)
            RAW = [sbt(st, f"raw{i}", [128, 12, 514], BF16) for i in range(3)]
            XT = [sbt(st, f"xt{i}", [128, D], F32) for i in range(2)]
            XN = [sbt(st, f"xn{i}", [128, D], BF16) for i in range(2)]
            SQ = sbt(st, "sq", [128, D], BF16)
            ST = [sbt(st, f"stt{i}", [128, 4], F32) for i in range(2)]
            HT = [sbt(st, f"ht{i}", [128, 8, 512], BF16) for i in range(2)]
            ZO = [sbt(st, "zo0", [128, D], BF16)] * 2
            QF = [sbt(st, "qf0", [128, 512], F32)] * 2
            QR = [sbt(st, "qr0", [128, 512], F32)] * 2
            QB = [sbt(st, f"qb{i}", [128, 512], BF16) for i in range(4)]
            QTt = [sbt(st, f"qtt{i}", [128, 4, 128], BF16) for i in range(2)]
            VO = [sbt(st, f"vo{i}", [128, 512], BF16) for i in range(2)]
            RP = [sbt(st, f"rp{i}", [128, 128], F32) for i in range(2)]
            XO = [sbt(st, f"xo{i}", [128, D], BF16) for i in range(2)]
            BO = [sbt(st, f"bo{i}", [128, 256], BF16) for i in range(2)]
            CV = [sbt(st, f"cv{i}", [128, 512], F32) for i in range(2)]
            CO = [sbt(st, f"co{i}", [128, 512], BF16) for i in range(2)]
            pT = [pst(st, f"pT{i}", [128, 8, 128], BF16) for i in range(2)]
            pF = [pst(st, f"pF{i}", [128, 512], F32) for i in range(4)]
            pM = pF
            pXT = pst(st, "pXT", [128, 16, 128], BF16)
            bWI, bCW, bDG = Buf(), Buf(), Buf()
            bRAW = [Buf() for _ in range(3)]
            bXT, bXN, bST, bHT = [Buf(), Buf()], [Buf(), Buf()], [Buf(), Buf()], [Buf(), Buf()]
            bSQ = Buf()
            bQTt, bVO, bRP = ([Buf(), Buf()] for _ in range(3))
            bQB = [Buf(), Buf(), Buf(), Buf()]
            qbc = [0]
            bZO, bQF, bQR = ([Buf()] * 2 for _ in range(3))
            bCV = [Buf(), Buf()]
            bCO10 = Buf()
            bXO, bBO, bCO = ([Buf(), Buf()] for _ in range(3))
            bpT = [Buf(), Buf()]
            bpF = [Buf() for _ in range(4)]
            bpM = bpF
            bpX = Buf()
            bDT = Buf()
            bS = {n: Buf(n) for n in ("XS", "BS", "BT", "CT", "ZS", "QT", "KT", "VS")}

            for kc in range(8):
                P.dma(WI[:, kc, :], w_in[kc * 128:(kc + 1) * 128, :], w=[bWI], q="pool")
            P.dma(CW[:], cw_pp, w=[bCW])
            P.dma(CB[:], cb_pp, w=[bCW])
            blocks = [(0, 2, True)] + [(2 + 4 * b, 4, False) for b in range(NL // 512)]
            nblk = len(blocks)
            tcount = [0]
            mc = [0]
            nslot = {}
            deferred = []

            def tick(force=False):
                for it in list(deferred):
                    it[0] -= 1
                    if it[0] <= 0 or force:
                        deferred.remove(it)
                        it[1]()

            def lat_tile(gt):
                return gt - 2

            def is_own(gt):
                return gt >= 2 and (gt - 2) < NTM

            def norm_tile(bi, tt):
                g0, ntile, isctx = blocks[bi]
                gt = g0 + tt
                s = tcount[0] % 2
                tcount[0] += 1
                nslot[(bi, tt)] = s
                P.dma(XT[s][:], xall[gt * 128:(gt + 1) * 128, :], w=[bXT[s]])
                P.act(lambda e, s=s: e.activation(out=SQ[:], in_=XT[s][:], func=AF.Square, accum_out=ST[s][:, 0:1]),
                      r=[bXT[s]], w=[bSQ, bST[s]])
                P.dve(lambda e, s=s: e.tensor_scalar(out=ST[s][:, 1:2], in0=ST[s][:, 0:1], scalar1=1.0 / D, scalar2=EPS, op0=ALU.mult, op1=ALU.add),
                      r=[bST[s]], w=[bST[s]])
                P.act(lambda e, s=s: e.activation(out=ST[s][:, 2:3], in_=ST[s][:, 1:2], func=AF.Sqrt), r=[bST[s]], w=[bST[s]])
                P.dve(lambda e, s=s: e.reciprocal(out=ST[s][:, 3:4], in_=ST[s][:, 2:3]), r=[bST[s]], w=[bST[s]])
                P.act(lambda e, s=s: e.activation(out=XN[s][:], in_=XT[s][:], func=AF.Copy, scale=ST[s][:, 3:4]),
                      r=[bXT[s], bST[s]], w=[bXN[s]])

            def trans_tile(bi, tt):
                g0, ntile, isctx = blocks[bi]
                hs = bi % 2
                s = nslot[(bi, tt)]
                for kc in range(8):
                    P.pe(lambda e, s=s, kc=kc: e.transpose(pT[s][:, kc, :], XN[s][:, kc * 128:(kc + 1) * 128], ident[:]),
                         r=[bXN[s], bK], w=[bpT[s]])
                go = 2 if isctx else 0
                for kc in range(8):
                    fn = (lambda e, s=s, kc=kc, tt=tt, go=go, hs=hs: e.activation(
                        out=HT[hs][:, kc, tt * 128:(tt + 1) * 128], in_=pT[s][:, kc, :], func=AF.Identity,
                        bias=GSH[:, go + 1, kc:kc + 1], scale=GSH[:, go, kc:kc + 1]))
                    fn2 = (lambda e, s=s, kc=kc, tt=tt, go=go, hs=hs: e.tensor_scalar(
                        out=HT[hs][:, kc, tt * 128:(tt + 1) * 128], in0=pT[s][:, kc, :],
                        scalar1=GSH[:, go, kc:kc + 1], scalar2=GSH[:, go + 1, kc:kc + 1], op0=ALU.mult, op1=ALU.add))
                    if kc % 2 == 0:
                        P.act(fn, r=[bpT[s], bK], w=[bHT[hs]])
                    else:
                        P.dve(fn2, r=[bpT[s], bK], w=[bHT[hs]])

            def front_fm(bi):
                g0, ntile, isctx = blocks[bi]
                ntok = ntile * 128
                hs = bi % 2
                anyown = any(is_own(g0 + tt) for tt in range(ntile))
                ncc = 12 if anyown else 10
                rs = bi % 3
                for cc in range(ncc):
                    f = mc[0] % 4
                    mc[0] += 1
                    for kc in range(8):
                        P.pe(lambda e, f=f, kc=kc, cc=cc, hs=hs, ntok=ntok: e.matmul(
                            pF[f][:, 0:ntok], lhsT=WI[:, kc, XBC0 + cc * 128:XBC0 + (cc + 1) * 128], rhs=HT[hs][:, kc, 0:ntok],
                            start=(kc == 0), stop=(kc == 7)), r=[bWI, bHT[hs]], w=[bpF[f]])
                    if cc % 2 == 0:
                        P.act(lambda e, f=f, cc=cc, rs=rs, ntok=ntok: e.copy(out=RAW[rs][:, cc, 1:1 + ntok], in_=pF[f][:, 0:ntok]),
                              r=[bpF[f]], w=[bRAW[rs]])
                    else:
                        P.dve(lambda e, f=f, cc=cc, rs=rs, ntok=ntok: e.tensor_copy(out=RAW[rs][:, cc, 1:1 + ntok], in_=pF[f][:, 0:ntok]),
                              r=[bpF[f]], w=[bRAW[rs]])
                first = isctx or bi == 1
                lastb = isctx or bi == nblk - 1
                if first:
                    P.pool(lambda e, rs=rs: e.memset(RAW[rs][:, :, 0:1], 0.0), w=[bRAW[rs]])
                else:
                    pr = (bi - 1) % 3
                    P.pool(lambda e, rs=rs, pr=pr: e.tensor_copy(out=RAW[pr][:, :, 513:514], in_=RAW[rs][:, :, 1:2]), r=[bRAW[rs]], w=[bRAW[pr]])
                    P.pool(lambda e, rs=rs, pr=pr: e.tensor_copy(out=RAW[rs][:, :, 0:1], in_=RAW[pr][:, :, 512:513]), r=[bRAW[pr]], w=[bRAW[rs]])
                if lastb:
                    P.pool(lambda e, rs=rs, ntok=ntok: e.memset(RAW[rs][:, :, ntok + 1:ntok + 2], 0.0), w=[bRAW[rs]])

            def front_tm_tile(bi, tt):
                g0, ntile, isctx = blocks[bi]
                hs = bi % 2
                gt = g0 + tt
                own = is_own(gt)
                groups = []
                if own:
                    groups += [("z", 0, 512), ("z", 512, 512)]
                groups += [("dt", DT0, 32)]
                if own:
                    groups += [("q", Q0, 512)]
                groups += [("k", K0, 512), ("v", V0, 512)]
                if not isctx:
                    rps = gt % 2
                    lt = lat_tile(gt)
                    P.dma(RP[rps][:], rope[lt * 128:(lt + 1) * 128, :], w=[bRP[rps]])
                for kind, c0, cw in groups:
                    m = mc[0] % 4
                    mc[0] += 1
                    for kc in range(8):
                        P.pe(lambda e, m=m, kc=kc, c0=c0, cw=cw, hs=hs, tt=tt: e.matmul(
                            pM[m][:, 0:cw], lhsT=HT[hs][:, kc, tt * 128:(tt + 1) * 128], rhs=WI[:, kc, c0:c0 + cw],
                            start=(kc == 0), stop=(kc == 7)), r=[bWI, bHT[hs]], w=[bpM[m]])
                    tick()
                    if kind == "z":
                        zs = gt % 2
                        P.act(lambda e, m=m, zs=zs, c0=c0: e.activation(out=ZO[zs][:, c0:c0 + 512], in_=pM[m][:], func=AF.Silu),
                              r=[bpM[m]], w=[bZO[zs]])
                        if c0 == 512:
                            lt = lat_tile(gt)
                            P.dma(ZS[lt * 128:(lt + 1) * 128, :], ZO[zs][:], r=[bZO[zs]], w=[bS["ZS"]], q="act")
                    elif kind == "dt":
                        P.dve(lambda e, m=m, gt=gt: e.tensor_copy(out=DTS[:, gt, :], in_=pM[m][:, 0:32]), r=[bpM[m]], w=[bDT])
                    elif kind == "v":
                        vs = gt % 2
                        P.dve(lambda e, m=m, vs=vs: e.tensor_copy(out=VO[vs][:], in_=pM[m][:]), r=[bpM[m]], w=[bVO[vs]])
                        P.dma(VS[gt * 128:(gt + 1) * 128, :], VO[vs][:], r=[bVO[vs]], w=[bS["VS"]], q="pool")
                    else:
                        qs = qbc[0] % 4
                        qbc[0] += 1
                        sc = 0.125 if kind == "q" else 1.0
                        if isctx:
                            P.act(lambda e, m=m, qs=qs: e.copy(out=QB[qs][:], in_=pM[m][:]), r=[bpM[m]], w=[bQB[qs]])
                        else:
                            rps = gt % 2
                            P.dve(lambda e, m=m, qs=qs, sc=sc: e.tensor_scalar(out=QF[0][:], in0=pM[m][:], scalar1=sc, scalar2=None, op0=ALU.mult),
                                  r=[bpM[m]], w=[bQF[0]])
                            qv = QF[0][:].rearrange("p (a g h x) -> p a g h x", a=8, g=2, h=2)
                            rv = QR[0][:].rearrange("p (a g h x) -> p a g h x", a=8, g=2, h=2)
                            sn = RP[rps][:, 64:128].rearrange("p (g h x) -> p g h x", g=2, h=2)
                            for g in range(2):
                                for hh in range(2):
                                    P.pool(lambda e, qs=qs, g=g, hh=hh, qv=qv, rv=rv, sn=sn: e.tensor_tensor(
                                        out=rv[:, :, g, hh, :], in0=qv[:, :, g, 1 - hh, :],
                                        in1=bc(sn[:, g, hh, :], [128, 8, 16], 1), op=ALU.mult),
                                        r=[bQF[0], bRP[rps]], w=[bQR[0]])
                            P.dve(lambda e, qs=qs, rps=rps: e.tensor_tensor(
                                out=QF[0][:].rearrange("p (a d) -> p a d", a=8), in0=QF[0][:].rearrange("p (a d) -> p a d", a=8),
                                in1=bc(RP[rps][:, 0:64], [128, 8, 64], 1), op=ALU.mult), r=[bQF[0], bRP[rps]], w=[bQF[0]])
                            P.dve(lambda e, qs=qs: e.tensor_tensor(out=QB[qs][:], in0=QF[0][:], in1=QR[0][:], op=ALU.add),
                                  r=[bQF[0], bQR[0]], w=[bQB[qs]])
                        def qk_tail(qs=qs, kind=kind, gt=gt):
                            ps = tcount[0] % 2
                            tcount[0] += 1
                            for h in range(4):
                                P.pe(lambda e, ps=ps, qs=qs, h=h: e.transpose(pT[ps][:, h, :], QB[qs][:, h * 128:(h + 1) * 128], ident[:]),
                                     r=[bQB[qs], bK], w=[bpT[ps]])
                            P.act(lambda e, ps=ps, qs=qs: e.copy(out=QTt[qs % 2][:], in_=pT[ps][:, 0:4, :]), r=[bpT[ps]], w=[bQTt[qs % 2]])
                            if kind == "q":
                                lt = lat_tile(gt)
                                P.dma(QT[:, :, lt * 128:(lt + 1) * 128].rearrange("h p t -> p h t"), QTt[qs % 2][:], r=[bQTt[qs % 2]], w=[bS["QT"]], q="act")
                            else:
                                P.dma(KT[:, :, gt * 128:(gt + 1) * 128].rearrange("h p t -> p h t"), QTt[qs % 2][:], r=[bQTt[qs % 2]], w=[bS["KT"]], q="act")
                        deferred.append([4, qk_tail])


            def back_ncc(bi):
                g0, ntile, isctx = blocks[bi]
                return 12 if any(is_own(g0 + tt) for tt in range(ntile)) else 10

            def back_conv(bi, ccs):
                g0, ntile, isctx = blocks[bi]
                ntok = ntile * 128
                rs = bi % 3
                for cc in ccs:
                    s = cc % 2
                    P.dve(lambda e, s=s, cc=cc, rs=rs, ntok=ntok: e.tensor_scalar(
                        out=CV[s][:, 0:ntok], in0=RAW[rs][:, cc, 1:1 + ntok], scalar1=CW[:, cc * 3 + 1:cc * 3 + 2], scalar2=CB[:, cc:cc + 1],
                        op0=ALU.mult, op1=ALU.add), r=[bRAW[rs], bCW], w=[bCV[s]])
                    for j in (0, 2):
                        P.dve(lambda e, s=s, cc=cc, rs=rs, ntok=ntok, j=j: e.scalar_tensor_tensor(
                            out=CV[s][:, 0:ntok], in0=RAW[rs][:, cc, j:j + ntok], scalar=CW[:, cc * 3 + j:cc * 3 + j + 1], in1=CV[s][:, 0:ntok],
                            op0=ALU.mult, op1=ALU.add), r=[bRAW[rs], bCW, bCV[s]], w=[bCV[s]])
                    if cc < 10:
                        P.act(lambda e, s=s, cc=cc, ntok=ntok: e.activation(out=CO10[:, cc, 0:ntok], in_=CV[s][:, 0:ntok], func=AF.Silu), r=[bCV[s]], w=[bCO10])
                        if cc >= 8:
                            P.dma(BT[(cc - 8) * 128:(cc - 7) * 128, g0 * 128:g0 * 128 + ntok], CO10[:, cc, 0:ntok], r=[bCO10], w=[bS["BT"]], q="act")
                    else:
                        P.act(lambda e, s=s, ntok=ntok: e.activation(out=CO[s][:, 0:ntok], in_=CV[s][:, 0:ntok], func=AF.Silu), r=[bCV[s]], w=[bCO[s]])
                        l0 = (g0 - 2) * 128
                        nn = min(ntok, NMIX - l0)
                        P.dma(CT[(cc - 10) * 128:(cc - 9) * 128, l0:l0 + nn], CO[s][:, 0:nn], r=[bCO[s]], w=[bS["CT"]], q="act")

            def back_trans(bi):
                g0, ntile, isctx = blocks[bi]
                for tt in range(ntile):
                    gt = g0 + tt
                    xs = gt % 2
                    for cx in range(10):
                        P.pe(lambda e, cx=cx, tt=tt: e.transpose(pXT[:, cx, :], CO10[:, cx, tt * 128:(tt + 1) * 128], ident[:]), r=[bCO10, bK], w=[bpX])
                    P.act(lambda e, xs=xs: e.copy(out=XO[xs][:].rearrange("p (k t) -> p k t", k=8), in_=pXT[:, 0:8, :]), r=[bpX], w=[bXO[xs]])
                    P.dve(lambda e, xs=xs: e.tensor_copy(out=BO[xs][:].rearrange("p (k t) -> p k t", k=2), in_=pXT[:, 8:10, :]), r=[bpX], w=[bBO[xs]])
                    P.dma(XS[gt * 128:(gt + 1) * 128, :], XO[xs][:], r=[bXO[xs]], w=[bS["XS"]], q="act")
                    P.dma(BS[gt * 128:(gt + 1) * 128, :], BO[xs][:], r=[bBO[xs]], w=[bS["BS"]], q="pool")

            for tt in range(blocks[0][1]):
                norm_tile(0, tt)
                trans_tile(0, tt)
            for bi in range(nblk):
                g0, ntile, isctx = blocks[bi]
                front_fm(bi)
                nn_ = blocks[bi + 1][1] if bi + 1 < nblk else 0
                per = (nn_ + ntile - 1) // ntile if nn_ else 0
                bj = 0 if bi == 0 else (bi - 1 if bi >= 2 else None)
                ncj = back_ncc(bj) if bj is not None else 0
                nsl = max(1, ntile - 1)
                cper = (ncj + nsl - 1) // nsl
                for tt in range(ntile):
                    share = list(range(tt * per, min(nn_, (tt + 1) * per)))
                    for t2 in share:
                        norm_tile(bi + 1, t2)
                    front_tm_tile(bi, tt)
                    for t2 in share:
                        trans_tile(bi + 1, t2)
                    if bj is not None:
                        back_conv(bj, list(range(tt * cper, min(ncj, (tt + 1) * cper))))
                if bj is not None:
                    back_trans(bj)
            tick(force=True)
            back_conv(nblk - 1, list(range(back_ncc(nblk - 1))))
            back_trans(nblk - 1)
            P.dve(lambda e: e.tensor_tensor(out=DTS[:], in0=DTS[:], in1=bc(SSDB[:, 32:64], [128, NTT, 32], 1), op=ALU.add), r=[bDT, bK], w=[bDT])
            P.act(lambda e: e.activation(out=DTS[:], in_=DTS[:], func=AF.Exp), r=[bDT], w=[bDT])
            P.act(lambda e: e.activation(out=DTS[:], in_=DTS[:], func=AF.Ln, bias=1.0), r=[bDT], w=[bDT])
            P.dve(lambda e: e.tensor_tensor(out=DAS[:], in0=DTS[:], in1=bc(SSDB[:, 0:32], [128, NTT, 32], 1), op=ALU.mult), r=[bDT, bK], w=[bDT])
            if debug:
                P.dma(DBG[:, 2048:2048 + NTT * 32], DTS[:].rearrange("p a b -> p (a b)"), r=[bDT])
            P.flush()
        if phases <= 1:
            return nc

        with ExitStack() as st:
            ACUM = sbt(st, "acum", [128, NTT, 32], F32)
            TOTB = sbt(st, "totb", [128, NTT, 32], F32)
            WDT = sbt(st, "wdt", [128, NTT, 32], F32)
            CD = TOTB
            EA = ACUM
            HSs = [sbt(st, f"hss{i}", [128, D], BF16) for i in range(2)]
            HSl = [sbt(st, f"hsl{i}", [128, D], BF16) for i in range(2)]
            HF = sbt(st, "hf", [128, D], F32)
            HB = sbt(st, "hb", [128, D], F32)
            HBb = sbt(st, "hbb", [128, D], BF16)
            XI = [sbt(st, f"xi{i}", [128, D], BF16) for i in range(2)]
            BI = [sbt(st, f"bi{i}", [128, 256], BF16) for i in range(2)]
            XW = [sbt(st, f"xw{i}", [128, D], BF16) for i in range(2)]
            BTc = [sbt(st, f"btc{i}", [128, 2, 128], BF16) for i in range(2)]
            CTc = [sbt(st, f"ctc{i}", [128, 2, 128], BF16) for i in range(2)]
            ZI = [sbt(st, f"zi{i}", [128, D], BF16) for i in range(2)]
            SM = [sbt(st, f"sm{i}", [128, 2, 2, 128], F32) for i in range(2)]
            RR = [sbt(st, f"rr{i}", [128, 2, 16, 128], BF16) for i in range(2)]
            DEC = [sbt(st, f"dec{i}", [128, 2, 16, 128], BF16) for i in range(2)]
            WT = sbt(st, "wt", [128, 2, 16, 128], BF16)
            XDT = sbt(st, "xdt", [128, 2, D], BF16)
            T1 = sbt(st, "t1", [128, D], F32)
            T2 = sbt(st, "t2", [128, D], F32)
            GN = sbt(st, "gn", [128, D], BF16)
            SQ2 = sbt(st, "sq2", [128, D], BF16)
            S2 = sbt(st, "s2", [128, 4], F32)
            YO = [sbt(st, f"yo{i}", [128, 8, 128], BF16) for i in range(2)]
            pG = [pst(st, f"pG{i}", [128, 512], F32) for i in range(2)]
            pY = pst(st, "pY", [128, D], F32)
            pO = pst(st, "pO", [128, D], F32)
            pC = pst(st, "pC", [128, D], F32)
            bAC, bHF, bHB, bHBb = Buf(), Buf(), Buf(), Buf()
            bHSs, bHSl = [Buf(), Buf()], [Buf(), Buf()]
            bHSd = Buf()
            bXI, bBI, bXW, bBTc, bCTc, bZI, bYO = ([Buf(), Buf()] for _ in range(7))
            bWT, bXDT, bT1, bT2, bGN, bS2, bSQ2 = (Buf() for _ in range(7))
            bSM, bRR, bDEC = ([Buf(), Buf()] for _ in range(3))
            bRRa = [[Buf() for _ in range(16)] for _ in range(2)]
            bpG = [Buf(), Buf()]
            bpY, bpO, bpC = Buf(), Buf(), Buf()
            bYT = Buf()

            tch = 32
            gi = 0
            for d in range(2):
                for t0 in range(0, NTT, tch):
                    tn = min(tch, NTT - t0)
                    for which, lhs, dst in ((0, TRI2[:, d, :], ACUM), (1, onesf[:], TOTB)):
                        g = gi % 2
                        gi += 1
                        P.pe(lambda e, g=g, lhs=lhs, t0=t0, tn=tn, d=d: e.matmul(
                            pG[g][:, 0:tn * 16].rearrange("p (a b) -> p a b", b=16), lhsT=lhs, rhs=DAS[:, t0:t0 + tn, d * 16:(d + 1) * 16],
                            start=True, stop=True), r=[bK], w=[bpG[g]])
                        P.dve(lambda e, g=g, dst=dst, t0=t0, tn=tn, d=d: e.tensor_copy(
                            out=dst[:, t0:t0 + tn, d * 16:(d + 1) * 16], in_=pG[g][:, 0:tn * 16].rearrange("p (a b) -> p a b", b=16)),
                            r=[bpG[g]], w=[bAC])
            P.dve(lambda e: e.tensor_tensor(out=WDT[:], in0=TOTB[:], in1=ACUM[:], op=ALU.subtract), r=[bAC], w=[bAC])
            P.act(lambda e: e.activation(out=WDT[:], in_=WDT[:], func=AF.Exp), r=[bAC], w=[bAC])
            P.dve(lambda e: e.tensor_tensor(out=WDT[:], in0=WDT[:], in1=DTS[:], op=ALU.mult), r=[bAC], w=[bAC])
            if debug:
                P.dma(DBG[:, 0:NTT * 32], ACUM[:].rearrange("p a b -> p (a b)"), r=[bAC])
            P.act(lambda e: e.activation(out=CD[:], in_=TOTB[:], func=AF.Exp), r=[bAC], w=[bAC])
            P.act(lambda e: e.activation(out=EA[:], in_=ACUM[:], func=AF.Exp), r=[bAC], w=[bAC])
            P.pool(lambda e: e.memset(HF[:], 0.0), w=[bHF])
            P.pool(lambda e: e.memset(HB[:], 0.0), w=[bHB])
            lc = [0]

            def load_xb(gt):
                s = lc[0] % 2
                lc[0] += 1
                P.dma(XI[s][:], XS[gt * 128:(gt + 1) * 128, :], w=[bXI[s]])
                P.dma(BI[s][:], BS[gt * 128:(gt + 1) * 128, :], w=[bBI[s]])
                return s

            def state_update(gt, s, d, H, bH):
                eng = P.dve if d == 0 else P.pool
                eng(lambda e, s=s, gt=gt, d=d: e.tensor_tensor(
                    out=XW[s][:].rearrange("p (h x) -> p h x", h=16), in0=XI[s][:].rearrange("p (h x) -> p h x", h=16),
                    in1=bc(WDT[:, gt, d * 16:(d + 1) * 16], [128, 16, 64], 2), op=ALU.mult), r=[bXI[s], bAC], w=[bXW[s]])
                for g in range(2):
                    P.pe(lambda e, s=s, g=g: e.matmul(pC[:, g * 512:(g + 1) * 512], lhsT=BI[s][:, g * 128:(g + 1) * 128],
                                                      rhs=XW[s][:, g * 512:(g + 1) * 512], start=True, stop=True),
                         r=[bBI[s], bXW[s]], w=[bpC])
                eng(lambda e, gt=gt, d=d, H=H: e.tensor_tensor(
                    out=H[:].rearrange("p (h x) -> p h x", h=16), in0=H[:].rearrange("p (h x) -> p h x", h=16),
                    in1=bc(CD[:, gt, d * 16:(d + 1) * 16], [128, 16, 64], 2), op=ALU.mult), r=[bH, bAC], w=[bH])
                P.dve(lambda e, H=H: e.tensor_tensor(out=H[:], in0=H[:], in1=pC[:], op=ALU.add), r=[bH, bpC], w=[bH])

            seqA = [0, 1] + [2 + lt for lt in range(NTM)]
            for gt in seqA:
                if gt >= 2:
                    lt = gt - 2
                    P.act(lambda e, lt=lt: e.copy(out=HSs[lt % 2][:], in_=HF[:]), r=[bHF], w=[bHSs[lt % 2]])
                    P.dma(HSd[lt], HSs[lt % 2][:], r=[bHSs[lt % 2]], w=[bHSd], q="act")
                    if lt == NTM - 1:
                        break
                s = load_xb(gt)
                state_update(gt, s, 0, HF, bHF)

            seqB = [1, 0] + [2 + lt for lt in range(NTL - 1, -1, -1)]
            pending = []
            mixseq = [gt for gt in seqB if gt >= 2 and gt - 2 < NTM]
            uof = {gt: i % 2 for i, gt in enumerate(mixseq)}

            def mix_head(gt):
                lt = gt - 2
                u = uof[gt]
                P.dve(lambda e, gt=gt: e.tensor_tensor(
                    out=RR[u][:, 0, :, :], in0=bc(DAS[:, gt, 0:16], [128, 16, 128], 2),
                    in1=bc(TRI2[:, 0, :], [128, 16, 128], 1), op=ALU.mult), r=[bK], w=[bRR[u]])
                for hh in range(16):
                    P.act(lambda e, gt=gt, hh=hh: e.activation(out=RR[u][:, 1, hh, :], in_=TRI2[:, 1, :], func=AF.Copy,
                                                               scale=DAS[:, gt, 16 + hh:17 + hh]), r=[bK], w=[bRRa[u][hh]])
                while len(pending) > 0:
                    pending.pop(0)()
                for g in range(2):
                    P.dma(BTc[u][:, g, :], BT[g * 128:(g + 1) * 128, gt * 128:(gt + 1) * 128], w=[bBTc[u]])
                    P.dma(CTc[u][:, g, :], CT[g * 128:(g + 1) * 128, lt * 128:(lt + 1) * 128], w=[bCTc[u]])
                P.dma(ZI[u][:], ZS[lt * 128:(lt + 1) * 128, :], w=[bZI[u]])
                P.dma(HSl[u][:], HSd[lt], w=[bHSl[u]])
                for g in range(2):
                    P.pe(lambda e, u=u, g=g: e.matmul(pG[0][:, g * 128:(g + 1) * 128], lhsT=BTc[u][:, g, :], rhs=CTc[u][:, g, :],
                                                      start=True, stop=True), r=[bBTc[u], bCTc[u]], w=[bpG[0]])
                for d in range(2):
                    P.dve(lambda e, d=d: e.tensor_tensor(out=SM[u][:, d, :, :], in0=pG[0][:, 0:256].rearrange("p (g l) -> p g l", g=2),
                                                         in1=bc(TRI2[:, d, :], [128, 2, 128], 1), op=ALU.mult), r=[bpG[0], bK], w=[bSM[u]])
                k = 1
                for d in range(2):
                    for hq in range(4):
                        g = k % 2
                        k += 1
                        P.pe(lambda e, g=g, d=d, hq=hq: e.matmul(pG[g][:].rearrange("p (h l) -> p h l", h=4), lhsT=UU[:, d, :],
                                                                 rhs=RR[u][:, d, hq * 4:(hq + 1) * 4, :], start=True, stop=True),
                             r=[bRR[u], bK] + (bRRa[u][hq * 4:(hq + 1) * 4] if d == 1 else []), w=[bpG[g]])
                        P.act(lambda e, g=g, d=d, hq=hq: e.activation(out=DEC[u][:, d, hq * 4:(hq + 1) * 4, :],
                                                                      in_=pG[g][:].rearrange("p (h l) -> p h l", h=4), func=AF.Exp),
                              r=[bpG[g]], w=[bDEC[u]])

            def mix_rest(gt, s):
                lt = gt - 2
                u = uof[gt]
                for d in range(2):
                    for g in range(2):
                        eng = P.dve
                        eng(lambda e, d=d, g=g: e.tensor_tensor(out=WT[:, d, g * 8:(g + 1) * 8, :], in0=DEC[u][:, d, g * 8:(g + 1) * 8, :],
                                                                in1=bc(SM[u][:, d, g, :], [128, 8, 128], 1), op=ALU.mult),
                            r=[bDEC[u], bSM[u]], w=[bWT])
                for d in range(2):
                    eng = P.pool
                    eng(lambda e, s=s, gt=gt, d=d: e.tensor_tensor(
                        out=XDT[:, d, :].rearrange("p (h x) -> p h x", h=16), in0=XI[s][:].rearrange("p (h x) -> p h x", h=16),
                        in1=bc(DTS[:, gt, d * 16:(d + 1) * 16], [128, 16, 64], 2), op=ALU.mult), r=[bXI[s], bK], w=[bXDT])
                for h in range(16):
                    for d in range(2):
                        P.pe(lambda e, h=h, d=d: e.matmul(pY[:, h * 64:(h + 1) * 64], lhsT=WT[:, d, h, :], rhs=XDT[:, d, h * 64:(h + 1) * 64],
                                                          start=(d == 0), stop=(d == 1)), r=[bWT, bXDT], w=[bpY])
                P.act(lambda e: e.copy(out=HBb[:], in_=HB[:]), r=[bHB], w=[bHBb])
                for d in range(2):
                    for g in range(2):
                        if d == 0:
                            P.pe(lambda e, u=u, g=g, lt=lt: e.matmul(pO[:, g * 512:(g + 1) * 512], lhsT=CTc[u][:, g, :],
                                                                     rhs=HSl[u][:, g * 512:(g + 1) * 512], start=True, stop=True),
                                 r=[bCTc[u], bHSl[u]], w=[bpO])
                        else:
                            P.pe(lambda e, u=u, g=g: e.matmul(pO[:, g * 512:(g + 1) * 512], lhsT=CTc[u][:, g, :],
                                                              rhs=HBb[:, g * 512:(g + 1) * 512], start=True, stop=True),
                                 r=[bCTc[u], bHBb], w=[bpO])
                    TT, bTT = (T1, bT1) if d == 0 else (T2, bT2)
                    P.dve(lambda e, TT=TT, gt=gt, d=d: e.tensor_tensor(
                        out=TT[:].rearrange("p (h x) -> p h x", h=16), in0=pO[:].rearrange("p (h x) -> p h x", h=16),
                        in1=bc(EA[:, gt, d * 16:(d + 1) * 16], [128, 16, 64], 2), op=ALU.mult), r=[bpO, bAC], w=[bTT])
                P.dve(lambda e: e.tensor_tensor(out=T1[:], in0=T1[:], in1=T2[:], op=ALU.add), r=[bT1, bT2], w=[bT1])
                P.pool(lambda e, s=s: e.tensor_tensor(out=T2[:].rearrange("p (h x) -> p h x", h=16), in0=XI[s][:].rearrange("p (h x) -> p h x", h=16),
                                                      in1=bc(SSDB[:, 64:80], [128, 16, 64], 2), op=ALU.mult), r=[bXI[s], bK, bT2], w=[bT2])
                P.dve(lambda e: e.tensor_tensor(out=T1[:], in0=T1[:], in1=T2[:], op=ALU.add), r=[bT1, bT2], w=[bT1])
                P.dve(lambda e: e.tensor_tensor(out=T1[:], in0=pY[:], in1=T1[:], op=ALU.add), r=[bpY, bT1], w=[bT1])
                P.dve(lambda e, u=u: e.tensor_tensor(out=T1[:], in0=T1[:], in1=ZI[u][:], op=ALU.mult), r=[bT1, bZI[u]], w=[bT1])
                P.act(lambda e: e.activation(out=SQ2[:], in_=T1[:], func=AF.Square, accum_out=S2[:, 0:1]), r=[bT1], w=[bSQ2, bS2])
                P.dve(lambda e: e.tensor_scalar(out=S2[:, 1:2], in0=S2[:, 0:1], scalar1=1.0 / D, scalar2=EPS, op0=ALU.mult, op1=ALU.add), r=[bS2], w=[bS2])
                P.act(lambda e: e.activation(out=S2[:, 2:3], in_=S2[:, 1:2], func=AF.Ln), r=[bS2], w=[bS2])
                P.act(lambda e: e.activation(out=S2[:, 3:4], in_=S2[:, 2:3], func=AF.Exp, scale=-0.5), r=[bS2], w=[bS2])
                P.act(lambda e: e.activation(out=GN[:], in_=T1[:], func=AF.Copy, scale=S2[:, 3:4]), r=[bT1, bS2], w=[bGN])
                def tail(u=u, lt=lt):
                    pTv = pG[1][:].bitcast(BF16).rearrange("p (k t) -> p k t", k=8)
                    for kc in range(8):
                        P.pe(lambda e, kc=kc, pTv=pTv: e.transpose(pTv[:, kc, :], GN[:, kc * 128:(kc + 1) * 128], ident[:]), r=[bGN, bK], w=[bpG[1]])
                    P.dve(lambda e, u=u, pTv=pTv: e.tensor_tensor(out=YO[u][:], in0=pTv, in1=bc(VEC[:, 16:24], [128, 8, 128], 2), op=ALU.mult),
                          r=[bpG[1], bK], w=[bYO[u]])
                    P.dma(YT2[lt, :, 0:8, :], YO[u][:], r=[bYO[u]], w=[bYT], q="pool")
                pending.append(tail)

            for gt in seqB:
                lt = gt - 2
                mix = gt >= 2 and lt < NTM
                s = load_xb(gt)
                if mix:
                    i = mixseq.index(gt)
                    if i == 0:
                        mix_head(gt)
                    if i + 1 < len(mixseq):
                        mix_head(mixseq[i + 1])
                    else:
                        while len(pending) > 0:
                            pending.pop(0)()
                    mix_rest(gt, s)
                if gt != 2:
                    state_update(gt, s, 1, HB, bHB)
            while len(pending) > 0:
                pending.pop(0)()
            P.flush()
        mid2.close()
        if phases <= 2:
            return nc

        with ExitStack() as st:
            KTs = sbt(st, "kts", [128, 4, T], BF16)
            VSs = sbt(st, "vss", [128, NTT, 512], BF16)
            QTs = [sbt(st, f"qts{i}", [128, 512], BF16) for i in range(2)]
            EE = [sbt(st, f"ee{i}", [128, 2, 512], BF16) for i in range(3)]
            R1 = sbt(st, "r1", [128, 512], F32)
            R2 = sbt(st, "r2", [128, 512], F32)
            OA = sbt(st, "oa", [128, 512], F32)
            OB = sbt(st, "ob", [128, 512], F32)
            OO = [sbt(st, f"oo{i}", [128, 512], BF16) for i in range(2)]
            pS = [pst(st, f"pS{i}", [128, 1024], F32) for i in range(2)]
            pO1 = pst(st, "pO1", [128, 512], F32)
            pO2 = pst(st, "pO2", [128, 512], F32)
            LS = sbt(st, "ls", [64, 512], F32)
            pL = pst(st, "pL", [128, 512], F32)
            pB = pst(st, "pB", [128, 512], F32)
            bLS, bpL, bpB = Buf(), Buf(), Buf()
            bKT, bVS = Buf(), Buf()
            bQTs, bOO = [Buf(), Buf()], [Buf(), Buf()]
            bEE = [Buf() for _ in range(3)]
            bpS = [Buf() for _ in range(2)]
            bpO1, bpO2, bpL1, bpL2, bR1, bR2, bOA, bOB, bYT2 = (Buf() for _ in range(9))
            for h in range(4):
                P.dma(KTs[:, h, :], KT[h], w=[bKT])
            vch = 8
            for t0 in range(0, NTT, vch):
                tn = min(vch, NTT - t0)
                P.dma(VSs[:, t0:t0 + tn, :], VS[t0 * 128:(t0 + tn) * 128, :].rearrange("(a p) c -> p a c", p=128), w=[bVS])
            qtiles = []
            q0 = 0
            while q0 < NMIX:
                qw = min(512, NMIX - q0)
                qtiles.append((q0, qw))
                q0 += qw
            steps = [(q0, qw, h, kt) for (q0, qw) in qtiles for h in range(4) for kt in range(NTT)]
            nst = len(steps)
            qslot = {}
            qn = 0
            for (q0, qw) in qtiles:
                for h in range(4):
                    qslot[(q0, h)] = qn % 2
                    qn += 1

            def emit_qk(i):
                q0, qw, h, kt = steps[i]
                qs = qslot[(q0, h)]
                if kt == 0:
                    P.dma(QTs[qs][:, 0:qw], QT[h, :, q0:q0 + qw], w=[bQTs[qs]])
                a = i % 2
                for m in range(2):
                    P.pe(lambda e, a=a, m=m, qs=qs, h=h, kt=kt, qw=qw: e.matmul(
                        pS[a][:, m * 512:m * 512 + qw], lhsT=KTs[m * 64:(m + 1) * 64, h, kt * 128:(kt + 1) * 128],
                        rhs=QTs[qs][m * 64:(m + 1) * 64, 0:qw], start=True, stop=True), r=[bKT, bQTs[qs]], w=[bpS[a]])

            att_def = []
            emit_qk(0)
            for i in range(nst):
                q0, qw, h, kt = steps[i]
                a = i % 2
                x = i % 3
                P.act(lambda e, a=a, x=x, qw=qw: e.activation(out=EE[x][:, :, 0:qw], in_=pS[a][:].rearrange("p (m q) -> p m q", m=2)[:, :, 0:qw],
                                                              func=AF.Exp), r=[bpS[a]], w=[bEE[x]])
                if i + 1 < nst:
                    emit_qk(i + 1)
                for m in range(2):
                    P.pe(lambda e, x=x, m=m, kt=kt, qw=qw: e.matmul(pL[32 * m:32 * m + 32, 0:qw], lhsT=onesb[:, 0:32], rhs=EE[x][:, m, 0:qw],
                                                                     start=(kt == 0), stop=(kt == NTT - 1), tile_position=(0, 32 * m)),
                         r=[bK, bEE[x]], w=[bpL])
                for m, pOx, bpOx in ((0, pO1, bpO1), (1, pO2, bpO2)):
                    P.pe(lambda e, x=x, m=m, pOx=pOx, kt=kt, h=h, qw=qw: e.matmul(pOx[:, 0:qw], lhsT=VSs[:, kt, h * 128:(h + 1) * 128], rhs=EE[x][:, m, 0:qw],
                                                                                 start=(kt == 0), stop=(kt == NTT - 1)), r=[bVS, bEE[x]], w=[bpOx])
                for it in list(att_def):
                    it[0] -= 1
                    if it[0] <= 0:
                        att_def.remove(it)
                        it[1]()
                if kt == NTT - 1:
                    qs = qslot[(q0, h)]
                    P.dve(lambda e, qw=qw: e.tensor_copy(out=OA[:, 0:qw], in_=pO1[:, 0:qw]), r=[bpO1], w=[bOA])
                    P.dve(lambda e, qw=qw: e.tensor_copy(out=OB[:, 0:qw], in_=pO2[:, 0:qw]), r=[bpO2], w=[bOB])
                    P.dve(lambda e, qw=qw: e.tensor_copy(out=LS[:, 0:qw], in_=pL[0:64, 0:qw]), r=[bpL], w=[bLS])

                    def epi(qw=qw, qs=qs, h=h, q0=q0):
                        for m, RR_, bRR_ in ((0, R1, bR1), (1, R2, bR2)):
                            P.pe(lambda e, m=m, qw=qw: e.matmul(pB[:, 0:qw], lhsT=onesf[32 * m:32 * m + 1, :], rhs=LS[32 * m:32 * m + 1, 0:qw], start=True, stop=True),
                                 r=[bLS, bK], w=[bpB])
                            P.dve(lambda e, RR_=RR_, qw=qw: e.reciprocal(out=RR_[:, 0:qw], in_=pB[:, 0:qw]), r=[bpB], w=[bRR_])
                        P.dve(lambda e, qw=qw: e.tensor_tensor(out=OA[:, 0:qw], in0=OA[:, 0:qw], in1=R1[:, 0:qw], op=ALU.mult), r=[bOA, bR1], w=[bOA])
                        P.dve(lambda e, qw=qw: e.tensor_tensor(out=OB[:, 0:qw], in0=OB[:, 0:qw], in1=R2[:, 0:qw], op=ALU.mult), r=[bOB, bR2], w=[bOB])
                        P.dve(lambda e, qw=qw, qs=qs: e.scalar_tensor_tensor(out=OO[qs][:, 0:qw], in0=OB[:, 0:qw], scalar=NLAM[:, 0:1], in1=OA[:, 0:qw],
                                                                             op0=ALU.mult, op1=ALU.add), r=[bOA, bOB, bK], w=[bOO[qs]])
                        P.dma(YT2[q0 // 128:(q0 + qw) // 128, :, 8 + h, :].rearrange("l p t -> p l t"), OO[qs][:, 0:qw].rearrange("p (l t) -> p l t", t=128), r=[bOO[qs]], w=[bYT2], q="pool")
                    att_def.append([4, epi])
            for it in att_def:
                it[1]()
            P.flush()
        if phases <= 3:
            return nc

        stU = top.enter_context(ExitStack())
        WU = sbt(stU, "wu", [128, 8, 2 * DFF], BF16)
        bWU = Buf()
        with ExitStack() as st:
            for kc in range(8):
                P.dma(WU[:, kc, :], w_up[kc * 128:(kc + 1) * 128, :], w=[bWU], q="pool")
            WO = sbt(st, "wo", [128, 12, D], BF16)
            YI = [sbt(st, f"yi{i}", [128, 12, 128], BF16) for i in range(3)]
            XT = [sbt(st, f"xt4{i}", [128, D], F32) for i in range(3)]
            XL = [sbt(st, f"xl{i}", [128, D], F32) for i in range(2)]
            XN = [sbt(st, f"xn4{i}", [128, D], BF16) for i in range(2)]
            SQ = sbt(st, "sq4", [128, D], BF16)
            ST = [sbt(st, f"st4{i}", [128, 4], F32) for i in range(2)]
            HO = [sbt(st, f"ho{i}", [128, 8, 128], BF16) for i in range(2)]
            ZC = sbt(st, "zc", [128, 8, 1], BF16)
            OSQ = sbt(st, "osq", [128, 4, 128], BF16)
            ORS = sbt(st, "ors", [128, 512], F32)
            pN = pst(st, "p4N", [128, 512], F32)
            bOSQ, bORS, bpN = Buf(), Buf(), Buf()
            pM = [pst(st, f"p4M{i}", [128, 512], F32) for i in range(2)]
            pT = [pst(st, f"p4T{i}", [128, 8, 128], BF16) for i in range(2)]
            bWO, bSQ, bZC, bH2, bXL1 = (Buf() for _ in range(5))
            bXL, bXN, bST, bHO, bpM, bpT = ([Buf(), Buf()] for _ in range(6))
            bYI, bXT = [Buf(), Buf(), Buf()], [Buf(), Buf(), Buf()]
            for kc in range(12):
                P.dma(WO[:, kc, :], w_out[kc * 128:(kc + 1) * 128, :], w=[bWO], q="pool")
            P.pool(lambda e: e.memset(ZC[:], 0.0), w=[bZC])
            P.dma(H2T[:, 0:1].rearrange("(k p) t -> p k t", p=128), ZC[:], r=[bZC], w=[bH2], slow=True)
            mi_ = [0]

            def stage1(lt):
                s = lt % 3
                P.dma(YI[s][:], YT2[lt], w=[bYI[s]])
                P.dma(XT[s][:], xall[CTX + lt * 128:CTX + (lt + 1) * 128, :], w=[bXT[s]])
                P.dve(lambda e, s=s: e.tensor_tensor(out=OSQ[:], in0=YI[s][:, 8:12, :], in1=YI[s][:, 8:12, :], op=ALU.mult), r=[bYI[s]], w=[bOSQ])
                P.pe(lambda e: e.matmul(pN[:], lhsT=onesb[:], rhs=OSQ[:].rearrange("p a t -> p (a t)"), start=True, stop=True), r=[bOSQ, bK], w=[bpN])
                P.dve(lambda e: e.tensor_scalar(out=ORS[:], in0=pN[:], scalar1=1.0 / 128, scalar2=EPS, op0=ALU.mult, op1=ALU.add), r=[bpN], w=[bORS])
                P.act(lambda e: e.activation(out=ORS[:], in_=ORS[:], func=AF.Ln), r=[bORS], w=[bORS])
                P.act(lambda e: e.activation(out=ORS[:], in_=ORS[:], func=AF.Exp, scale=-0.5), r=[bORS], w=[bORS])
                P.dve(lambda e, s=s: e.scalar_tensor_tensor(out=YI[s][:, 8:12, :], in0=YI[s][:, 8:12, :], scalar=SGS[:, 0:1],
                                                            in1=ORS[:].rearrange("p (a t) -> p a t", a=4), op0=ALU.mult, op1=ALU.mult),
                      r=[bYI[s], bORS, bK], w=[bYI[s]])

            def stage2(lt):
                s3 = lt % 3
                s = lt % 2
                for cg in range(2):
                    m = mi_[0] % 2
                    mi_[0] += 1
                    for kc in range(12):
                        P.pe(lambda e, m=m, s=s, kc=kc, cg=cg: e.matmul(pM[m][:], lhsT=YI[s3][:, kc, :], rhs=WO[:, kc, cg * 512:(cg + 1) * 512],
                                                                        start=(kc == 0), stop=(kc == 11)), r=[bYI[s3], bWO], w=[bpM[m]])
                    P.dve(lambda e, m=m, s=s, cg=cg: e.tensor_tensor(out=XL[s][:, cg * 512:(cg + 1) * 512], in0=pM[m][:], in1=MOD2B[:, cg * 512:(cg + 1) * 512],
                                                                     op=ALU.mult), r=[bpM[m], bK], w=[bXL[s]])
                P.dve(lambda e, s=s: e.tensor_tensor(out=XL[s][:], in0=XL[s][:], in1=XT[s3][:], op=ALU.add), r=[bXL[s], bXT[s3]], w=[bXL[s]])
                P.dma(XL1[lt * 128:(lt + 1) * 128, :], XL[s][:], r=[bXL[s]], w=[bXL1], q="pool")
                P.act(lambda e, s=s: e.activation(out=SQ[:], in_=XL[s][:], func=AF.Square, accum_out=ST[s][:, 0:1]), r=[bXL[s]], w=[bSQ, bST[s]])
                P.dve(lambda e, s=s: e.tensor_scalar(out=ST[s][:, 1:2], in0=ST[s][:, 0:1], scalar1=1.0 / D, scalar2=EPS, op0=ALU.mult, op1=ALU.add),
                      r=[bST[s]], w=[bST[s]])
                P.act(lambda e, s=s: e.activation(out=ST[s][:, 2:3], in_=ST[s][:, 1:2], func=AF.Ln), r=[bST[s]], w=[bST[s]])
                P.act(lambda e, s=s: e.activation(out=ST[s][:, 3:4], in_=ST[s][:, 2:3], func=AF.Exp, scale=-0.5), r=[bST[s]], w=[bST[s]])
                P.act(lambda e, s=s: e.activation(out=XN[s][:], in_=XL[s][:], func=AF.Copy, scale=ST[s][:, 3:4]), r=[bXL[s], bST[s]], w=[bXN[s]])

            def stageB(lt):
                s = lt % 2
                for kc in range(8):
                    P.pe(lambda e, s=s, kc=kc: e.transpose(pT[s][:, kc, :], XN[s][:, kc * 128:(kc + 1) * 128], ident[:]), r=[bXN[s], bK], w=[bpT[s]])
                for kc in range(8):
                    if kc % 2 == 0:
                        P.act(lambda e, s=s, kc=kc: e.activation(out=HO[s][:, kc, :], in_=pT[s][:, kc, :], func=AF.Identity,
                                                                 bias=GSH[:, 5, kc:kc + 1], scale=GSH[:, 4, kc:kc + 1]), r=[bpT[s], bK], w=[bHO[s]])
                    else:
                        P.dve(lambda e, s=s, kc=kc: e.tensor_scalar(out=HO[s][:, kc, :], in0=pT[s][:, kc, :], scalar1=GSH[:, 4, kc:kc + 1],
                                                                    scalar2=GSH[:, 5, kc:kc + 1], op0=ALU.mult, op1=ALU.add), r=[bpT[s], bK], w=[bHO[s]])
                P.dma(H2T[:, 1 + lt * 128:1 + (lt + 1) * 128].rearrange("(k p) t -> p k t", p=128), HO[s][:], r=[bHO[s]], w=[bH2], q="pool")

            for lt in range(NTM + 2):
                if lt < NTM:
                    stage1(lt)
                if 1 <= lt <= NTM:
                    stage2(lt - 1)
                if lt >= 2:
                    stageB(lt - 2)
            P.flush()

        with ExitStack() as st:
            WD = sbt(st, "wd", [128, 22, D], BF16)
            FW = sbt(st, "fw", [128, 66], F32)
            FB = sbt(st, "fb", [128, 22], F32)
            HW = [sbt(st, "hw0", [128, 8, 512], BF16)] * 2
            C1 = [sbt(st, f"c1{i}", [128, 512], F32) for i in range(2)]
            SG = C1
            GT_ = sbt(st, "gt", [128, 22, 512], BF16)
            XI4 = [sbt(st, f"xi4{i}", [128, D], F32) for i in range(2)]
            X2 = XI4
            SQ = sbt(st, "sq5", [128, D], BF16)
            ST = [sbt(st, f"st5{i}", [128, 4], F32) for i in range(2)]
            OUT = [sbt(st, f"out{i}", [128, D], F32) for i in range(2)]
            pGa = [pst(st, f"pGa{i}", [128, 512], F32) for i in range(2)]
            pVa = [pst(st, f"pVa{i}", [128, 512], F32) for i in range(2)]
            pD = [pst(st, f"pD{i}", [128, 512], F32) for i in range(2)]
            bWD, bFW, bGT, bSQ, bOUTD = (Buf() for _ in range(5))
            bC1, bXI4, bST, bOUT, bpGa, bpVa, bpD = ([Buf(), Buf()] for _ in range(7))
            bHW = [Buf()] * 2
            bSG, bX2 = bC1, bXI4
            for kc in range(22):
                P.dma(WD[:, kc, :], w_down[kc * 128:(kc + 1) * 128, :], w=[bWD], q="pool")
            P.dma(FW[:], fw_pp, w=[bFW])
            P.dma(FB[:], fb_pp, w=[bFW])
            nb4 = (NOWN + 509) // 510
            ti = 0
            di = 0
            for b in range(nb4):
                t0 = b * 510
                nt = min(510, NOWN - t0)
                hs = b % 2
                P.dma(HW[hs][:, :, 0:nt + 2], H2T[:, t0:t0 + nt + 2].rearrange("(k p) t -> p k t", p=128), w=[bHW[hs]])
                for cc in range(22):
                    f = cc % 2
                    for kc in range(8):
                        P.pe(lambda e, f=f, kc=kc, cc=cc, hs=hs, nt=nt: e.matmul(pGa[f][:, 0:nt + 2], lhsT=WU[:, kc, DFF + cc * 128:DFF + (cc + 1) * 128],
                                                                                 rhs=HW[hs][:, kc, 0:nt + 2], start=(kc == 0), stop=(kc == 7)),
                             r=[bWU, bHW[hs]], w=[bpGa[f]])
                    for kc in range(8):
                        P.pe(lambda e, f=f, kc=kc, cc=cc, hs=hs, nt=nt: e.matmul(pVa[f][:, 0:nt], lhsT=WU[:, kc, cc * 128:(cc + 1) * 128],
                                                                                 rhs=HW[hs][:, kc, 1:nt + 1], start=(kc == 0), stop=(kc == 7)),
                             r=[bWU, bHW[hs]], w=[bpVa[f]])
                    P.act(lambda e, f=f, cc=cc, nt=nt: e.activation(out=C1[f][:, 0:nt], in_=pGa[f][:, 1:nt + 1], func=AF.Identity,
                                                                    bias=FB[:, cc:cc + 1], scale=FW[:, cc * 3 + 1:cc * 3 + 2]), r=[bpGa[f], bFW], w=[bC1[f]])
                    for j in (0, 2):
                        P.dve(lambda e, f=f, cc=cc, nt=nt, j=j: e.scalar_tensor_tensor(out=C1[f][:, 0:nt], in0=pGa[f][:, j:j + nt],
                                                                                       scalar=FW[:, cc * 3 + j:cc * 3 + j + 1], in1=C1[f][:, 0:nt],
                                                                                       op0=ALU.mult, op1=ALU.add), r=[bpGa[f], bFW, bC1[f]], w=[bC1[f]])
                    P.act(lambda e, f=f, nt=nt: e.activation(out=SG[f][:, 0:nt], in_=C1[f][:, 0:nt], func=AF.Silu), r=[bC1[f]], w=[bSG[f]])
                    P.dve(lambda e, f=f, cc=cc, nt=nt: e.tensor_tensor(out=GT_[:, cc, 0:nt], in0=SG[f][:, 0:nt], in1=pVa[f][:, 0:nt], op=ALU.mult),
                          r=[bSG[f], bpVa[f]], w=[bGT])
                ts = 0
                while ts < nt:
                    tw = min(128, nt - ts)
                    s = ti % 2
                    ti += 1
                    g0 = t0 + ts
                    P.dma(XI4[s][0:tw, :], XL1[g0:g0 + tw, :], w=[bXI4[s]])
                    for cg in range(2):
                        m = di % 2
                        di += 1
                        for cc in range(22):
                            P.pe(lambda e, m=m, cc=cc, ts=ts, tw=tw, cg=cg: e.matmul(pD[m][0:tw, :], lhsT=GT_[:, cc, ts:ts + tw],
                                                                                     rhs=WD[:, cc, cg * 512:(cg + 1) * 512], start=(cc == 0), stop=(cc == 21)),
                                 r=[bGT, bWD], w=[bpD[m]])
                        P.dve(lambda e, m=m, s=s, tw=tw, cg=cg: e.tensor_tensor(out=OUT[s][0:tw, cg * 512:(cg + 1) * 512], in0=pD[m][0:tw, :],
                                                                                in1=MOD5B[0:tw, cg * 512:(cg + 1) * 512], op=ALU.mult),
                              r=[bpD[m], bK], w=[bOUT[s]])
                    P.pool(lambda e, s=s, tw=tw: e.tensor_tensor(out=X2[s][0:tw, :], in0=OUT[s][0:tw, :], in1=XI4[s][0:tw, :], op=ALU.add),
                           r=[bOUT[s], bXI4[s]], w=[bX2[s]])
                    P.act(lambda e, s=s, tw=tw: e.activation(out=SQ[0:tw, :], in_=X2[s][0:tw, :], func=AF.Square, accum_out=ST[s][0:tw, 0:1]),
                          r=[bX2[s]], w=[bSQ, bST[s]])
                    P.dve(lambda e, s=s, tw=tw: e.tensor_scalar(out=ST[s][0:tw, 1:2], in0=ST[s][0:tw, 0:1], scalar1=1.0 / D, scalar2=EPS,
                                                                op0=ALU.mult, op1=ALU.add), r=[bST[s]], w=[bST[s]])
                    P.act(lambda e, s=s, tw=tw: e.activation(out=ST[s][0:tw, 2:3], in_=ST[s][0:tw, 1:2], func=AF.Sqrt), r=[bST[s]], w=[bST[s]])
                    P.dve(lambda e, s=s, tw=tw: e.reciprocal(out=ST[s][0:tw, 3:4], in_=ST[s][0:tw, 2:3]), r=[bST[s]], w=[bST[s]])
                    P.dve(lambda e, s=s, tw=tw: e.scalar_tensor_tensor(out=OUT[s][0:tw, :], in0=X2[s][0:tw, :], scalar=ST[s][0:tw, 3:4],
                                                                       in1=FGB[0:tw, :], op0=ALU.mult, op1=ALU.mult),
                          r=[bX2[s], bST[s], bK], w=[bOUT[s]])
                    P.dma(out[g0:g0 + tw, :], OUT[s][0:tw, :], r=[bOUT[s]], w=[bOUTD], q="pool")
                    ts += tw
            P.flush(final=True)
    return nc


def rope_table(NL, flip):
    t = np.arange(NL)
    pos = (NL - 1 - t) if flip else t
    row = (pos // 64).astype(np.float32)
    col = (pos % 64).astype(np.float32)
    inv = np.power(np.float32(10000.0), -np.arange(0, 32, 2, dtype=np.float32) / np.float32(32)).astype(np.float32)
    ar = row[:, None] * inv
    ac = col[:, None] * inv
    cos = np.concatenate([np.cos(ar), np.cos(ar), np.cos(ac), np.cos(ac)], axis=1)
    sin = np.concatenate([-np.sin(ar), np.sin(ar), -np.sin(ac), np.sin(ac)], axis=1)
    return np.ascontiguousarray(np.concatenate([cos, sin], axis=1).astype(np.float32))


def pp(v, n):
    return np.ascontiguousarray(np.asarray(v, np.float32).reshape(n, 128).T)


def core_inputs(inp, b, half, NL):
    flip = half == 1
    f32 = lambda a: np.ascontiguousarray(np.asarray(a, np.float32))
    x = np.asarray(inp["x"])[b]
    cx = np.asarray(inp["ctx"])[b]
    if flip:
        x = x[::-1]
        cx = cx[::-1]
    xall = f32(np.concatenate([cx, x], axis=0))
    w_in = np.asarray(inp["w_in"])[0]
    order = [1, 0] if flip else [0, 1]
    if flip:
        w_in = np.concatenate([w_in[:, :DT0], w_in[:, DT0 + 16:DT0 + 32], w_in[:, DT0:DT0 + 16], w_in[:, DT0 + 32:]], axis=1)
    cw = np.asarray(inp["conv_w"])[0]
    fw = np.asarray(inp["ffn_conv_w"])[0]
    if flip:
        cw = cw[::-1]
        fw = fw[::-1]
    cw_pp = np.stack([pp(cw[j], 12) for j in range(3)], axis=2).reshape(128, 36)
    fw_pp = np.stack([pp(fw[j], 22) for j in range(3)], axis=2).reshape(128, 66)
    ssd_row = np.concatenate([np.asarray(inp[k])[0][order].reshape(-1) for k in ("a_log", "dt_bias", "d_skip")])[None, :]
    lam_row = np.concatenate([np.asarray(inp[k])[0] for k in ("lam_q1", "lam_k1", "lam_q2", "lam_k2")])[None, :]
    vec = np.zeros((128, 32), np.float32)
    vec[:, 0:8] = pp(np.asarray(inp["norm1_g"])[0], 8)
    vec[:, 8:16] = pp(np.asarray(inp["norm2_g"])[0], 8)
    vec[:, 16:24] = pp(np.asarray(inp["ssd_norm_g"])[0], 8)
    vec[:, 24] = np.asarray(inp["subln_g"])[0]
    c_pp = np.concatenate([pp(np.asarray(inp["c"])[b], 8), pp(np.asarray(inp["c_ctx"]), 8)], axis=1)
    return {
        "xall": xall, "c_pp": f32(c_pp), "w_mod": f32(np.asarray(inp["w_mod"])[0]), "b_mod": f32(np.asarray(inp["b_mod"])[0][None, :]),
        "vec_pp": vec, "w_in": f32(w_in), "cw_pp": f32(cw_pp), "cb_pp": pp(np.asarray(inp["conv_b"])[0], 12),
        "cb_row": f32(np.asarray(inp["conv_b"])[0][None, :]), "ssd_row": f32(ssd_row), "lam_row": f32(lam_row),
        "w_out": f32(np.asarray(inp["w_out"])[0]), "w_up": f32(np.asarray(inp["w_up"])[0]), "fw_pp": f32(fw_pp),
        "fb_pp": pp(np.asarray(inp["ffn_conv_b"])[0], 22), "w_down": f32(np.asarray(inp["w_down"])[0]),
        "fg_row": f32(np.asarray(inp["final_g"])[None, :]), "rope": rope_table(NL, flip),
    }


_NC_CACHE = {}


def kernel(**inputs):
    NL = int(np.asarray(inputs["x"]).shape[1])
    B = int(np.asarray(inputs["x"]).shape[0])
    if NL not in _NC_CACHE:
        _NC_CACHE[NL] = build(NL)
    nc = _NC_CACHE[NL]
    in_maps = [core_inputs(inputs, c // 2, c % 2, NL) for c in range(2 * B)]
    res = run_bass_kernel_spmd(nc, in_maps, core_ids=list(range(2 * B)))
    out = np.empty((B, NL, D), np.float32)
    h = NL // 2
    for c in range(2 * B):
        o = np.asarray(res.results[c]["out"], np.float32)
        if c % 2 == 0:
            out[c // 2, :h] = o
        else:
            out[c // 2, h:] = o[::-1]
    return out
```

```python
import math
from contextlib import ExitStack

import numpy as np
import concourse.bass as bass
import concourse.mybir as mybir
from concourse.bass_utils import run_bass_kernel_spmd

F32 = mybir.dt.float32
BF16 = mybir.dt.bfloat16
AF = mybir.ActivationFunctionType
ALU = mybir.AluOpType

D = 1024
CTX = 256
INW = 4128
XBC0, DT0, Q0, K0, V0 = 1024, 2560, 2592, 3104, 3616
DFF = 2816
EPS = 1e-6


class Buf:
    __slots__ = ("name", "w", "r")

    def __init__(self, name=""):
        self.name = name
        self.w = None
        self.r = []


class Op:
    __slots__ = ("eng", "fn", "dma", "deps", "sig", "sem", "val", "waits", "ph")


ENGS = ("pe", "act", "dve", "pool", "sp")
SEM_CAP = 30000
DMA_K = 16


class Prog:
    def __init__(self, nc, stack, same_eng_sync=("pool", "dve", "act")):
        self.nc = nc
        self.stack = stack
        self.same = same_eng_sync
        self.ops = []
        self.ph = 0
        self.nsig = {e: 0 for e in ENGS}
        self.ndma = {e: 0 for e in ENGS}
        self.sems = {e: [] for e in ENGS}
        self.dsems = {e: [stack.enter_context(nc.semaphore(f"d_{e}_{i}")) for i in range(DMA_K)]
                      for e in ("sp", "pool", "act")}
        self.waited = {e: {} for e in ENGS}
        self.bar = stack.enter_context(nc.semaphore("bar"))
        self.dfin = {e: {} for e in ENGS}
        self.count = {e: 0 for e in ENGS}

    def add(self, eng, fn, reads=(), writes=(), dma=False):
        op = Op()
        op.eng, op.fn, op.dma, op.sig, op.sem, op.val, op.waits, op.ph = eng, fn, dma, False, None, 0, [], self.ph
        deps = set()
        for b in reads:
            if b.w is not None:
                deps.add(b.w)
        for b in writes:
            if b.w is not None:
                deps.add(b.w)
            deps.update(b.r)
        for b in reads:
            if (not dma) and eng in ("pe", "act", "dve"):
                b.r = [r for r in b.r if r.dma or r.eng != eng]
            b.r.append(op)
        for b in writes:
            b.w = op
            b.r = []
        deps.discard(op)
        op.deps = [d for d in deps if d.ph == self.ph]
        self.ops.append(op)
        return op

    def pe(self, fn, r=(), w=()):
        return self.add("pe", fn, r, w)

    def act(self, fn, r=(), w=()):
        return self.add("act", fn, r, w)

    def dve(self, fn, r=(), w=()):
        return self.add("dve", fn, r, w)

    def pool(self, fn, r=(), w=()):
        return self.add("pool", fn, r, w)

    def dma(self, out, in_, r=(), w=(), q="sp", slow=False):
        if slow:
            return self.add(q, lambda e: e.dma_start(out=out, in_=in_, allow_slow_non_contiguous=True), r, w, dma=True)
        return self.add(q, lambda e: e.dma_start(out=out, in_=in_), r, w, dma=True)

    def _sem(self, e, m):
        i = m // SEM_CAP
        while len(self.sems[e]) <= i:
            self.sems[e].append(self.stack.enter_context(self.nc.semaphore(f"s_{e}_{len(self.sems[e])}")))
        return self.sems[e][i], (m % SEM_CAP) + 1

    def flush(self, final=False):
        ops = self.ops
        self.ops = []
        per = {e: [o for o in ops if o.eng == e] for e in ENGS}
        for op in ops:
            for d in op.deps:
                if d.dma:
                    continue
                if d.eng != op.eng or op.dma or (op.eng in self.same):
                    d.sig = True
        last = {}
        for e in ENGS:
            comp = [o for o in per[e] if not o.dma]
            if comp:
                comp[-1].sig = True
                last[e] = comp[-1]
        for e in ENGS:
            for o in per[e]:
                if o.dma:
                    j = self.ndma[e]
                    self.ndma[e] += 1
                    o.sem = self.dsems[e][j % DMA_K]
                    o.val = 16 * (j // DMA_K + 1)
                    self.dfin[e][o.sem] = o.val
                elif o.sig:
                    o.sem, o.val = self._sem(e, self.nsig[e])
                    self.nsig[e] += 1
        for e in ENGS:
            waited = self.waited[e]
            for o in per[e]:
                need = {}
                if o.dma and o.val > 16:
                    need[o.sem] = o.val - 16
                for d in o.deps:
                    if d.sem is None:
                        continue
                    if (not d.dma) and d.eng == e and not (o.dma or (e in self.same)):
                        continue
                    if need.get(d.sem, 0) < d.val:
                        need[d.sem] = d.val
                for s, v in need.items():
                    if waited.get(s, 0) >= v:
                        continue
                    waited[s] = v
                    o.waits.append((s, v))
            self.count[e] += len(per[e])
        self.ph += 1
        target = 5 * self.ph
        bar = self.bar

        def run(eng, e):
            for o in per[e]:
                for s, v in o.waits:
                    eng.wait_ge(s, v)
                ins = o.fn(eng)
                if o.dma:
                    ins.then_inc(o.sem, 16)
                elif o.sig:
                    ins.then_inc(o.sem, 1)
            if e in last:
                eng.wait_ge(last[e].sem, last[e].val)
            for s, v in self.dfin[e].items():
                eng.wait_ge(s, v)
            eng.sem_inc(bar, 1)
            eng.wait_ge(bar, target)

        with self.nc.Block() as block:
            @block.tensor
            def _(eng):
                run(eng, "pe")

            @block.scalar
            def _(eng):
                run(eng, "act")

            @block.vector
            def _(eng):
                run(eng, "dve")

            @block.gpsimd
            def _(eng):
                run(eng, "pool")

            @block.sync
            def _(eng):
                run(eng, "sp")


def bc(ap, shape, axis):
    return ap.unsqueeze(axis).to_broadcast(shape)


def build(NL, debug=False, phases=5):
    NOWN = NL // 2
    NMIX = NOWN + 128
    T = CTX + NL
    NTL, NTM, NTT = NL // 128, NMIX // 128, T // 128
    nc = bass.Bass("TRN2", target_bir_lowering=False)

    def din(name, shape, dt=F32):
        return nc.dram_tensor(name, shape, dt, kind="ExternalInput").ap()

    xall = din("xall", [T, D])
    c_pp = din("c_pp", [128, 16])
    w_mod = din("w_mod", [D, 6 * D])
    b_mod = din("b_mod", [1, 6 * D])
    vec_pp = din("vec_pp", [128, 32])
    w_in = din("w_in", [D, INW])
    cw_pp = din("cw_pp", [128, 36])
    cb_pp = din("cb_pp", [128, 12])
    cb_row = din("cb_row", [1, 1536])
    ssd_row = din("ssd_row", [1, 96])
    lam_row = din("lam_row", [1, 256])
    w_out = din("w_out", [1536, D])
    w_up = din("w_up", [D, 2 * DFF])
    fw_pp = din("fw_pp", [128, 66])
    fb_pp = din("fb_pp", [128, 22])
    w_down = din("w_down", [DFF, D])
    fg_row = din("fg_row", [1, D])
    rope = din("rope", [NL, 128])
    out = nc.dram_tensor("out", [NOWN, D], F32, kind="ExternalOutput").ap()

    skind = "ExternalOutput" if debug else "Internal"

    def dscr(name, shape, dt):
        return nc.dram_tensor(name, shape, dt, kind=skind).ap()

    XS = dscr("XS", [T, D], BF16)
    BS = dscr("BS", [T, 256], BF16)
    BT = dscr("BT", [256, T], BF16)
    CT = dscr("CT", [256, NMIX], BF16)
    ZS = dscr("ZS", [NMIX, D], BF16)
    QT = dscr("QT", [4, 128, NMIX], BF16)
    KT = dscr("KT", [4, 128, T], BF16)
    VS = dscr("VS", [T, 512], BF16)
    YT2 = dscr("YT2", [NTM, 128, 12, 128], BF16)
    XL1 = dscr("XL1", [NMIX, D], F32)
    H2T = dscr("H2T", [D, NMIX + 2], BF16)
    HSd = dscr("HSd", [NTM, 128, D], BF16)
    DBG = dscr("DBG", [128, 4096], F32)

    with ExitStack() as top:
        P = Prog(nc, top)

        def sbt(st, name, shape, dt):
            return st.enter_context(nc.sbuf_tensor(name, shape, dt))

        def pst(st, name, shape, dt):
            return st.enter_context(nc.psum_tensor(name, shape, dt))

        ident = sbt(top, "ident", [128, 128], BF16)
        identf = sbt(top, "identf", [128, 128], F32)
        onesf = sbt(top, "onesf", [128, 128], F32)
        onesb = sbt(top, "onesb", [128, 128], BF16)
        TRI2 = sbt(top, "tri2", [128, 2, 128], F32)
        UU = sbt(top, "uu", [128, 2, 128], BF16)
        VEC = sbt(top, "vec", [128, 32], F32)
        MOD5B = sbt(top, "mod5b", [128, D], F32)
        PPM = sbt(top, "ppm", [128, 6, 8], F32)
        GSH = sbt(top, "gsh", [128, 6, 8], F32)
        SSDB = sbt(top, "ssdb", [128, 96], F32)
        LAMB = sbt(top, "lamb", [128, 256], F32)
        NLAM = sbt(top, "nlam", [128, 4], F32)
        FGB = sbt(top, "fgb", [128, D], F32)
        SGS = sbt(top, "sgs", [128, 1], F32)
        mid1 = top.enter_context(ExitStack())
        MOD2B = sbt(mid1, "mod2b", [128, D], F32)
        mid2 = top.enter_context(ExitStack())
        DTS = sbt(mid2, "dts", [128, NTT, 32], F32)
        DAS = sbt(mid2, "das", [128, NTT, 32], F32)
        bK = Buf("const")

        with ExitStack() as st:
            WM = sbt(st, "wm", [128, 8, 6 * D], BF16)
            BMr = sbt(st, "bmr", [1, 6 * D], BF16)
            CP = sbt(st, "cp", [128, 16], F32)
            SC = sbt(st, "sc", [128, 16], F32)
            SCB = sbt(st, "scb", [128, 16, 128], BF16)
            MODL = sbt(st, "modl", [128, 6 * D], F32)
            MODC = sbt(st, "modc", [128, 2 * D], F32)
            tmp = sbt(st, "tmp0", [128, 128], F32)
            pm = [pst(st, f"pm{i}", [128, 512], F32) for i in range(2)]
            ptr = pst(st, "ptr", [128, 4, 128], F32)
            bWM, bBM, bSC, bMOD, bPT = Buf(), Buf(), Buf(), Buf(), Buf()
            bpm = [Buf(), Buf()]
            for kc in range(8):
                P.dma(WM[:, kc, :], w_mod[kc * 128:(kc + 1) * 128, :], w=[bWM], q="pool")
            P.dma(BMr[:], b_mod, w=[bBM], q="pool")
            P.dma(CP[:], c_pp, w=[bSC])
            P.dma(VEC[:], vec_pp, w=[bK])
            P.dma(SSDB[:], ssd_row.partition_broadcast(128), w=[bK])
            P.dma(LAMB[:], lam_row.partition_broadcast(128), w=[bK])
            P.dma(FGB[:], fg_row.partition_broadcast(128), w=[bK])
            P.pool(lambda e: e.memset(identf[:], 1.0), w=[bK])
            P.pool(lambda e: e.memset(onesf[:], 1.0), w=[bK])
            P.pool(lambda e: e.memset(TRI2[:], 1.0), w=[bK])
            P.pool(lambda e: e.memset(tmp[:], 1.0), w=[bK])

            def asel(o, i, op, sgn=1):
                return lambda e: e.affine_select(out=o, in_=i, pattern=[[-sgn, 128]], compare_op=op, fill=0.0,
                                                 base=0, channel_multiplier=sgn)
            P.pool(asel(identf[:], identf[:], ALU.is_equal), r=[bK], w=[bK])
            P.pool(asel(TRI2[:, 0, :], TRI2[:, 0, :], ALU.is_ge, -1), r=[bK], w=[bK])
            P.pool(asel(TRI2[:, 1, :], TRI2[:, 1, :], ALU.is_ge, 1), r=[bK], w=[bK])
            P.dve(lambda e: e.tensor_copy(out=ident[:], in_=identf[:]), r=[bK], w=[bK])
            P.dve(lambda e: e.tensor_copy(out=onesb[:], in_=onesf[:]), r=[bK], w=[bK])
            P.dve(lambda e: e.tensor_scalar(out=UU[:], in0=TRI2[:], scalar1=-1.0, scalar2=1.0, op0=ALU.mult, op1=ALU.add),
                  r=[bK], w=[bK])
            P.dve(lambda e: e.tensor_tensor(out=LAMB[:, 0:64], in0=LAMB[:, 0:64], in1=LAMB[:, 64:128], op=ALU.mult), r=[bK], w=[bK])
            P.dve(lambda e: e.tensor_tensor(out=LAMB[:, 128:192], in0=LAMB[:, 128:192], in1=LAMB[:, 192:256], op=ALU.mult), r=[bK], w=[bK])
            P.dve(lambda e: e.tensor_reduce(out=NLAM[:, 0:1], in_=LAMB[:, 0:64], axis=mybir.AxisListType.X, op=ALU.add), r=[bK], w=[bK])
            P.dve(lambda e: e.tensor_reduce(out=NLAM[:, 1:2], in_=LAMB[:, 128:192], axis=mybir.AxisListType.X, op=ALU.add), r=[bK], w=[bK])
            P.act(lambda e: e.activation(out=NLAM[:, 2:4], in_=NLAM[:, 0:2], func=AF.Exp), r=[bK], w=[bK])
            P.dve(lambda e: e.scalar_tensor_tensor(out=NLAM[:, 0:1], in0=NLAM[:, 3:4], scalar=-0.2, in1=NLAM[:, 2:3],
                                                   op0=ALU.add, op1=ALU.subtract), r=[bK], w=[bK])
            P.act(lambda e: e.activation(out=SSDB[:, 0:32], in_=SSDB[:, 0:32], func=AF.Exp), r=[bK], w=[bK])
            P.dve(lambda e: e.tensor_scalar(out=SSDB[:, 0:32], in0=SSDB[:, 0:32], scalar1=-1.0, scalar2=None, op0=ALU.mult), r=[bK], w=[bK])
            P.dve(lambda e: e.tensor_tensor(out=SSDB[:, 64:80], in0=SSDB[:, 64:80], in1=SSDB[:, 80:96], op=ALU.add), r=[bK], w=[bK])
            P.dve(lambda e: e.tensor_scalar(out=SGS[:], in0=VEC[:, 24:25], scalar1=0.8, scalar2=None, op0=ALU.mult), r=[bK], w=[bK])
            P.act(lambda e: e.activation(out=SC[:], in_=CP[:], func=AF.Silu), r=[bSC], w=[bSC])
            P.dve(lambda e: e.tensor_copy(out=SCB[:], in_=bc(SC[:], [128, 16, 128], 2)), r=[bSC], w=[bSC])
            cnt = 0
            for v, ncg, dst in ((0, 12, MODL), (1, 4, MODC)):
                for cg in range(ncg):
                    pp, bp = pm[cnt % 2], bpm[cnt % 2]
                    cnt += 1
                    for kc in range(8):
                        P.pe(lambda e, pp=pp, v=v, kc=kc, cg=cg: e.matmul(pp[:], lhsT=SCB[:, v * 8 + kc, :], rhs=WM[:, kc, cg * 512:(cg + 1) * 512],
                                                                          start=(kc == 0), stop=False), r=[bSC, bWM], w=[bp])
                    P.pe(lambda e, pp=pp, cg=cg: e.matmul(pp[:], lhsT=onesb[0:1, :], rhs=BMr[0:1, cg * 512:(cg + 1) * 512], start=False, stop=True),
                         r=[bK, bBM], w=[bp])
                    P.dve(lambda e, pp=pp, dst=dst, cg=cg: e.tensor_copy(out=dst[:, cg * 512:(cg + 1) * 512], in_=pp[:]), r=[bp], w=[bMOD])
            P.dve(lambda e: e.tensor_copy(out=MOD2B[:], in_=MODL[:, 2 * D:3 * D]), r=[bMOD], w=[bK])
            P.dve(lambda e: e.tensor_copy(out=MOD5B[:], in_=MODL[:, 5 * D:6 * D]), r=[bMOD], w=[bK])
            for i, (src, m) in enumerate(((MODL, 0), (MODL, 1), (MODL, 3), (MODL, 4), (MODC, 0), (MODC, 1))):
                for half in range(2):
                    for k4 in range(4):
                        kc = half * 4 + k4
                        P.pe(lambda e, src=src, m=m, kc=kc, k4=k4: e.transpose(ptr[:, k4, :], src[:, m * D + kc * 128:m * D + (kc + 1) * 128], identf[:]),
                             r=[bMOD, bK], w=[bPT])
                    P.dve(lambda e, i=i, half=half: e.tensor_copy(out=PPM[:, i, half * 4:(half + 1) * 4], in_=ptr[:, :, 0]), r=[bPT], w=[bK])
            for o, msc, msh, g0 in ((0, 1, 0, 0), (2, 5, 4, 0), (4, 3, 2, 8)):
                P.dve(lambda e, o=o, msc=msc, g0=g0: e.scalar_tensor_tensor(out=GSH[:, o, :], in0=PPM[:, msc, :], scalar=1.0, in1=VEC[:, g0:g0 + 8],
                                                                           op0=ALU.add, op1=ALU.mult), r=[bK], w=[bK])
                P.dve(lambda e, o=o, msh=msh: e.tensor_copy(out=GSH[:, o + 1, :], in_=PPM[:, msh, :]), r=[bK], w=[bK])
            if debug:
                P.dma(DBG[:, 0:48], GSH[:].rearrange("p a b -> p (a b)"), r=[bK])
                P.dma(DBG[:, 48:52], NLAM[:], r=[bK])
                P.dma(DBG[:, 64:160], SSDB[:], r=[bK])
                P.dma(DBG[:, 1024:2048], MOD2B[:], r=[bK])
            P.flush()
        if phases <= 0:
            return nc

        with ExitStack() as st:
            WI = sbt(st, "wi", [128, 8, INW], BF16)
            CW = sbt(st, "cw", [128, 36], F32)
            CB = sbt(st, "cb", [128, 12], F32)
            CO10 = sbt(st, "co10", [128, 10, 512], BF16)
            RAW = [sbt(st, f"raw{i}", [128, 12, 514], BF16) for i in range(3)]
            XT = [sbt(st, f"xt{i}", [128, D], F32) for i in range(2)]
            XN = [sbt(st, f"xn{i}", [128, D], BF16) for i in range(2)]
            SQ = sbt(st, "sq", [128, D], BF16)
            ST = [sbt(st, f"stt{i}", [128, 4], F32) for i in range(2)]
            HT = [sbt(st, f"ht{i}", [128, 8, 512], BF16) for i in range(2)]
            ZO = [sbt(st, "zo0", [128, D], BF16)] * 2
            QF = [sbt(st, "qf0", [128, 512], F32)] * 2
            QR = [sbt(st, "qr0", [128, 512], F32)] * 2
            QB = [sbt(st, f"qb{i}", [128, 512], BF16) for i in range(4)]
            QTt = [sbt(st, f"qtt{i}", [128, 4, 128], BF16) for i in range(2)]
            VO = [sbt(st, f"vo{i}", [128, 512], BF16) for i in range(2)]
            RP = [sbt(st, f"rp{i}", [128, 128], F32) for i in range(2)]
            XO = [sbt(st, f"xo{i}", [128, D], BF16) for i in range(2)]
            BO = [sbt(st, f"bo{i}", [128, 256], BF16) for i in range(2)]
            CV = [sbt(st, f"cv{i}", [128, 512], F32) for i in range(2)]
            CO = [sbt(st, f"co{i}", [128, 512], BF16) for i in range(2)]
            pT = [pst(st, f"pT{i}", [128, 8, 128], BF16) for i in range(2)]
            pF = [pst(st, f"pF{i}", [128, 512], F32) for i in range(4)]
            pM = pF
            pXT = pst(st, "pXT", [128, 16, 128], BF16)
            bWI, bCW, bDG = Buf(), Buf(), Buf()
            bRAW = [Buf() for _ in range(3)]
            bXT, bXN, bST, bHT = [Buf(), Buf()], [Buf(), Buf()], [Buf(), Buf()], [Buf(), Buf()]
            bSQ = Buf()
            bQTt, bVO, bRP = ([Buf(), Buf()] for _ in range(3))
            bQB = [Buf(), Buf(), Buf(), Buf()]
            qbc = [0]
            bZO, bQF, bQR = ([Buf()] * 2 for _ in range(3))
            bCV = [Buf(), Buf()]
            bCO10 = Buf()
            bXO, bBO, bCO = ([Buf(), Buf()] for _ in range(3))
            bpT = [Buf(), Buf()]
            bpF = [Buf() for _ in range(4)]
            bpM = bpF
            bpX = Buf()
            bDT = Buf()
            bS = {n: Buf(n) for n in ("XS", "BS", "BT", "CT", "ZS", "QT", "KT", "VS")}

            for kc in range(8):
                P.dma(WI[:, kc, :], w_in[kc * 128:(kc + 1) * 128, :], w=[bWI], q="pool")
            P.dma(CW[:], cw_pp, w=[bCW])
            P.dma(CB[:], cb_pp, w=[bCW])
            blocks = [(0, 2, True)] + [(2 + 4 * b, 4, False) for b in range(NL // 512)]
            nblk = len(blocks)
            tcount = [0]
            mc = [0]
            nslot = {}
            deferred = []

            def tick(force=False):
                for it in list(deferred):
                    it[0] -= 1
                    if it[0] <= 0 or force:
                        deferred.remove(it)
                        it[1]()

            def lat_tile(gt):
                return gt - 2

            def is_own(gt):
                return gt >= 2 and (gt - 2) < NTM

            def norm_tile(bi, tt):
                g0, ntile, isctx = blocks[bi]
                gt = g0 + tt
                s = tcount[0] % 2
                tcount[0] += 1
                nslot[(bi, tt)] = s
                P.dma(XT[s][:], xall[gt * 128:(gt + 1) * 128, :], w=[bXT[s]])
                P.act(lambda e, s=s: e.activation(out=SQ[:], in_=XT[s][:], func=AF.Square, accum_out=ST[s][:, 0:1]),
                      r=[bXT[s]], w=[bSQ, bST[s]])
                P.dve(lambda e, s=s: e.tensor_scalar(out=ST[s][:, 1:2], in0=ST[s][:, 0:1], scalar1=1.0 / D, scalar2=EPS, op0=ALU.mult, op1=ALU.add),
                      r=[bST[s]], w=[bST[s]])
                P.act(lambda e, s=s: e.activation(out=ST[s][:, 2:3], in_=ST[s][:, 1:2], func=AF.Sqrt), r=[bST[s]], w=[bST[s]])
                P.dve(lambda e, s=s: e.reciprocal(out=ST[s][:, 3:4], in_=ST[s][:, 2:3]), r=[bST[s]], w=[bST[s]])
                P.act(lambda e, s=s: e.activation(out=XN[s][:], in_=XT[s][:], func=AF.Copy, scale=ST[s][:, 3:4]),
                      r=[bXT[s], bST[s]], w=[bXN[s]])

            def trans_tile(bi, tt):
                g0, ntile, isctx = blocks[bi]
                hs = bi % 2
                s = nslot[(bi, tt)]
                for kc in range(8):
                    P.pe(lambda e, s=s, kc=kc: e.transpose(pT[s][:, kc, :], XN[s][:, kc * 128:(kc + 1) * 128], ident[:]),
                         r=[bXN[s], bK], w=[bpT[s]])
                go = 2 if isctx else 0
                for kc in range(8):
                    fn = (lambda e, s=s, kc=kc, tt=tt, go=go, hs=hs: e.activation(
                        out=HT[hs][:, kc, tt * 128:(tt + 1) * 128], in_=pT[s][:, kc, :], func=AF.Identity,
                        bias=GSH[:, go + 1, kc:kc + 1], scale=GSH[:, go, kc:kc + 1]))
                    fn2 = (lambda e, s=s, kc=kc, tt=tt, go=go, hs=hs: e.tensor_scalar(
                        out=HT[hs][:, kc, tt * 128:(tt + 1) * 128], in0=pT[s][:, kc, :],
                        scalar1=GSH[:, go, kc:kc + 1], scalar2=GSH[:, go + 1, kc:kc + 1], op0=ALU.mult, op1=ALU.add))
                    if kc % 2 == 0:
                        P.act(fn, r=[bpT[s], bK], w=[bHT[hs]])
                    else:
                        P.dve(fn2, r=[bpT[s], bK], w=[bHT[hs]])

            def front_fm(bi):
                g0, ntile, isctx = blocks[bi]
                ntok = ntile * 128
                hs = bi % 2
                anyown = any(is_own(g0 + tt) for tt in range(ntile))
                ncc = 12 if anyown else 10
                rs = bi % 3
                for cc in range(ncc):
                    f = mc[0] % 4
                    mc[0] += 1
                    for kc in range(8):
                        P.pe(lambda e, f=f, kc=kc, cc=cc, hs=hs, ntok=ntok: e.matmul(
                            pF[f][:, 0:ntok], lhsT=WI[:, kc, XBC0 + cc * 128:XBC0 + (cc + 1) * 128], rhs=HT[hs][:, kc, 0:ntok],
                            start=(kc == 0), stop=(kc == 7)), r=[bWI, bHT[hs]], w=[bpF[f]])
                    if cc % 2 == 0:
                        P.act(lambda e, f=f, cc=cc, rs=rs, ntok=ntok: e.copy(out=RAW[rs][:, cc, 1:1 + ntok], in_=pF[f][:, 0:ntok]),
                              r=[bpF[f]], w=[bRAW[rs]])
                    else:
                        P.dve(lambda e, f=f, cc=cc, rs=rs, ntok=ntok: e.tensor_copy(out=RAW[rs][:, cc, 1:1 + ntok], in_=pF[f][:, 0:ntok]),
                              r=[bpF[f]], w=[bRAW[rs]])
                first = isctx or bi == 1
                lastb = isctx or bi == nblk - 1
                if first:
                    P.pool(lambda e, rs=rs: e.memset(RAW[rs][:, :, 0:1], 0.0), w=[bRAW[rs]])
                else:
                    pr = (bi - 1) % 3
                    P.pool(lambda e, rs=rs, pr=pr: e.tensor_copy(out=RAW[pr][:, :, 513:514], in_=RAW[rs][:, :, 1:2]), r=[bRAW[rs]], w=[bRAW[pr]])
                    P.pool(lambda e, rs=rs, pr=pr: e.tensor_copy(out=RAW[rs][:, :, 0:1], in_=RAW[pr][:, :, 512:513]), r=[bRAW[pr]], w=[bRAW[rs]])
                if lastb:
                    P.pool(lambda e, rs=rs, ntok=ntok: e.memset(RAW[rs][:, :, ntok + 1:ntok + 2], 0.0), w=[bRAW[rs]])

            def front_tm_tile(bi, tt):
                g0, ntile, isctx = blocks[bi]
                hs = bi % 2
                gt = g0 + tt
                own = is_own(gt)
                groups = []
                if own:
                    groups += [("z", 0, 512), ("z", 512, 512)]
                groups += [("dt", DT0, 32)]
                if own:
                    groups += [("q", Q0, 512)]
                groups += [("k", K0, 512), ("v", V0, 512)]
                if not isctx:
                    rps = gt % 2
                    lt = lat_tile(gt)
                    P.dma(RP[rps][:], rope[lt * 128:(lt + 1) * 128, :], w=[bRP[rps]])
                for kind, c0, cw in groups:
                    m = mc[0] % 4
                    mc[0] += 1
                    for kc in range(8):
                        P.pe(lambda e, m=m, kc=kc, c0=c0, cw=cw, hs=hs, tt=tt: e.matmul(
                            pM[m][:, 0:cw], lhsT=HT[hs][:, kc, tt * 128:(tt + 1) * 128], rhs=WI[:, kc, c0:c0 + cw],
                            start=(kc == 0), stop=(kc == 7)), r=[bWI, bHT[hs]], w=[bpM[m]])
                    tick()
                    if kind == "z":
                        zs = gt % 2
                        P.act(lambda e, m=m, zs=zs, c0=c0: e.activation(out=ZO[zs][:, c0:c0 + 512], in_=pM[m][:], func=AF.Silu),
                              r=[bpM[m]], w=[bZO[zs]])
                        if c0 == 512:
                            lt = lat_tile(gt)
                            P.dma(ZS[lt * 128:(lt + 1) * 128, :], ZO[zs][:], r=[bZO[zs]], w=[bS["ZS"]], q="act")
                    elif kind == "dt":
                        P.dve(lambda e, m=m, gt=gt: e.tensor_copy(out=DTS[:, gt, :], in_=pM[m][:, 0:32]), r=[bpM[m]], w=[bDT])
                    elif kind == "v":
                        vs = gt % 2
                        P.dve(lambda e, m=m, vs=vs: e.tensor_copy(out=VO[vs][:], in_=pM[m][:]), r=[bpM[m]], w=[bVO[vs]])
                        P.dma(VS[gt * 128:(gt + 1) * 128, :], VO[vs][:], r=[bVO[vs]], w=[bS["VS"]], q="pool")
                    else:
                        qs = qbc[0] % 4
                        qbc[0] += 1
                        sc = 0.125 if kind == "q" else 1.0
                        if isctx:
                            P.act(lambda e, m=m, qs=qs: e.copy(out=QB[qs][:], in_=pM[m][:]), r=[bpM[m]], w=[bQB[qs]])
                        else:
                            rps = gt % 2
                            P.dve(lambda e, m=m, qs=qs, sc=sc: e.tensor_scalar(out=QF[0][:], in0=pM[m][:], scalar1=sc, scalar2=None, op0=ALU.mult),
                                  r=[bpM[m]], w=[bQF[0]])
                            qv = QF[0][:].rearrange("p (a g h x) -> p a g h x", a=8, g=2, h=2)
                            rv = QR[0][:].rearrange("p (a g h x) -> p a g h x", a=8, g=2, h=2)
                            sn = RP[rps][:, 64:128].rearrange("p (g h x) -> p g h x", g=2, h=2)
                            for g in range(2):
                                for hh in range(2):
                                    P.pool(lambda e, qs=qs, g=g, hh=hh, qv=qv, rv=rv, sn=sn: e.tensor_tensor(
                                        out=rv[:, :, g, hh, :], in0=qv[:, :, g, 1 - hh, :],
                                        in1=bc(sn[:, g, hh, :], [128, 8, 16], 1), op=ALU.mult),
                                        r=[bQF[0], bRP[rps]], w=[bQR[0]])
                            P.dve(lambda e, qs=qs, rps=rps: e.tensor_tensor(
                                out=QF[0][:].rearrange("p (a d) -> p a d", a=8), in0=QF[0][:].rearrange("p (a d) -> p a d", a=8),
                                in1=bc(RP[rps][:, 0:64], [128, 8, 64], 1), op=ALU.mult), r=[bQF[0], bRP[rps]], w=[bQF[0]])
                            P.dve(lambda e, qs=qs: e.tensor_tensor(out=QB[qs][:], in0=QF[0][:], in1=QR[0][:], op=ALU.add),
                                  r=[bQF[0], bQR[0]], w=[bQB[qs]])
                        def qk_tail(qs=qs, kind=kind, gt=gt):
                            ps = tcount[0] % 2
                            tcount[0] += 1
                            for h in range(4):
                                P.pe(lambda e, ps=ps, qs=qs, h=h: e.transpose(pT[ps][:, h, :], QB[qs][:, h * 128:(h + 1) * 128], ident[:]),
                                     r=[bQB[qs], bK], w=[bpT[ps]])
                            P.act(lambda e, ps=ps, qs=qs: e.copy(out=QTt[qs % 2][:], in_=pT[ps][:, 0:4, :]), r=[bpT[ps]], w=[bQTt[qs % 2]])
                            if kind == "q":
                                lt = lat_tile(gt)
                                P.dma(QT[:, :, lt * 128:(lt + 1) * 128].rearrange("h p t -> p h t"), QTt[qs % 2][:], r=[bQTt[qs % 2]], w=[bS["QT"]], q="act")
                            else:
                                P.dma(KT[:, :, gt * 128:(gt + 1) * 128].rearrange("h p t -> p h t"), QTt[qs % 2][:], r=[bQTt[qs % 2]], w=[bS["KT"]], q="act")
                        deferred.append([4, qk_tail])


            def back_ncc(bi):
                g0, ntile, isctx = blocks[bi]
                return 12 if any(is_own(g0 + tt) for tt in range(ntile)) else 10

            def back_conv(bi, ccs):
                g0, ntile, isctx = blocks[bi]
                ntok = ntile * 128
                rs = bi % 3
                for cc in ccs:
                    s = cc % 2
                    P.dve(lambda e, s=s, cc=cc, rs=rs, ntok=ntok: e.tensor_scalar(
                        out=CV[s][:, 0:ntok], in0=RAW[rs][:, cc, 1:1 + ntok], scalar1=CW[:, cc * 3 + 1:cc * 3 + 2], scalar2=CB[:, cc:cc + 1],
                        op0=ALU.mult, op1=ALU.add), r=[bRAW[rs], bCW], w=[bCV[s]])
                    for j in (0, 2):
                        P.dve(lambda e, s=s, cc=cc, rs=rs, ntok=ntok, j=j: e.scalar_tensor_tensor(
                            out=CV[s][:, 0:ntok], in0=RAW[rs][:, cc, j:j + ntok], scalar=CW[:, cc * 3 + j:cc * 3 + j + 1], in1=CV[s][:, 0:ntok],
                            op0=ALU.mult, op1=ALU.add), r=[bRAW[rs], bCW, bCV[s]], w=[bCV[s]])
                    if cc < 10:
                        P.act(lambda e, s=s, cc=cc, ntok=ntok: e.activation(out=CO10[:, cc, 0:ntok], in_=CV[s][:, 0:ntok], func=AF.Silu), r=[bCV[s]], w=[bCO10])
                        if cc >= 8:
                            P.dma(BT[(cc - 8) * 128:(cc - 7) * 128, g0 * 128:g0 * 128 + ntok], CO10[:, cc, 0:ntok], r=[bCO10], w=[bS["BT"]], q="act")
                    else:
                        P.act(lambda e, s=s, ntok=ntok: e.activation(out=CO[s][:, 0:ntok], in_=CV[s][:, 0:ntok], func=AF.Silu), r=[bCV[s]], w=[bCO[s]])
                        l0 = (g0 - 2) * 128
                        nn = min(ntok, NMIX - l0)
                        P.dma(CT[(cc - 10) * 128:(cc - 9) * 128, l0:l0 + nn], CO[s][:, 0:nn], r=[bCO[s]], w=[bS["CT"]], q="act")

            def back_trans(bi):
                g0, ntile, isctx = blocks[bi]
                for tt in range(ntile):
                    gt = g0 + tt
                    xs = gt % 2
                    for cx in range(10):
                        P.pe(lambda e, cx=cx, tt=tt: e.transpose(pXT[:, cx, :], CO10[:, cx, tt * 128:(tt + 1) * 128], ident[:]), r=[bCO10, bK], w=[bpX])
                    P.act(lambda e, xs=xs: e.copy(out=XO[xs][:].rearrange("p (k t) -> p k t", k=8), in_=pXT[:, 0:8, :]), r=[bpX], w=[bXO[xs]])
                    P.dve(lambda e, xs=xs: e.tensor_copy(out=BO[xs][:].rearrange("p (k t) -> p k t", k=2), in_=pXT[:, 8:10, :]), r=[bpX], w=[bBO[xs]])
                    P.dma(XS[gt * 128:(gt + 1) * 128, :], XO[xs][:], r=[bXO[xs]], w=[bS["XS"]], q="act")
                    P.dma(BS[gt * 128:(gt + 1) * 128, :], BO[xs][:], r=[bBO[xs]], w=[bS["BS"]], q="pool")

            for tt in range(blocks[0][1]):
                norm_tile(0, tt)
                trans_tile(0, tt)
            for bi in range(nblk):
                g0, ntile, isctx = blocks[bi]
                front_fm(bi)
                nn_ = blocks[bi + 1][1] if bi + 1 < nblk else 0
                per = (nn_ + ntile - 1) // ntile if nn_ else 0
                bj = 0 if bi == 0 else (bi - 1 if bi >= 2 else None)
                ncj = back_ncc(bj) if bj is not None else 0
                nsl = max(1, ntile - 1)
                cper = (ncj + nsl - 1) // nsl
                for tt in range(ntile):
                    share = list(range(tt * per, min(nn_, (tt + 1) * per)))
                    for t2 in share:
                        norm_tile(bi + 1, t2)
                    front_tm_tile(bi, tt)
                    for t2 in share:
                        trans_tile(bi + 1, t2)
                    if bj is not None:
                        back_conv(bj, list(range(tt * cper, min(ncj, (tt + 1) * cper))))
                if bj is not None:
                    back_trans(bj)
            tick(force=True)
            back_conv(nblk - 1, list(range(back_ncc(nblk - 1))))
            back_trans(nblk - 1)
            P.dve(lambda e: e.tensor_tensor(out=DTS[:], in0=DTS[:], in1=bc(SSDB[:, 32:64], [128, NTT, 32], 1), op=ALU.add), r=[bDT, bK], w=[bDT])
            P.act(lambda e: e.activation(out=DTS[:], in_=DTS[:], func=AF.Exp), r=[bDT], w=[bDT])
            P.act(lambda e: e.activation(out=DTS[:], in_=DTS[:], func=AF.Ln, bias=1.0), r=[bDT], w=[bDT])
            P.dve(lambda e: e.tensor_tensor(out=DAS[:], in0=DTS[:], in1=bc(SSDB[:, 0:32], [128, NTT, 32], 1), op=ALU.mult), r=[bDT, bK], w=[bDT])
            if debug:
                P.dma(DBG[:, 2048:2048 + NTT * 32], DTS[:].rearrange("p a b -> p (a b)"), r=[bDT])
            P.flush()
        if phases <= 1:
            return nc

        with ExitStack() as st:
            ACUM = sbt(st, "acum", [128, NTT, 32], F32)
            TOTB = sbt(st, "totb", [128, NTT, 32], F32)
            WDT = sbt(st, "wdt", [128, NTT, 32], F32)
            CD = TOTB
            EA = ACUM
            HSs = [sbt(st, f"hss{i}", [128, D], BF16) for i in range(2)]
            HSl = [sbt(st, f"hsl{i}", [128, D], BF16) for i in range(2)]
            HF = sbt(st, "hf", [128, D], F32)
            HB = sbt(st, "hb", [128, D], F32)
            HBb = sbt(st, "hbb", [128, D], BF16)
            XI = [sbt(st, f"xi{i}", [128, D], BF16) for i in range(2)]
            BI = [sbt(st, f"bi{i}", [128, 256], BF16) for i in range(2)]
            XW = [sbt(st, f"xw{i}", [128, D], BF16) for i in range(2)]
            BTc = [sbt(st, f"btc{i}", [128, 2, 128], BF16) for i in range(2)]
            CTc = [sbt(st, f"ctc{i}", [128, 2, 128], BF16) for i in range(2)]
            ZI = [sbt(st, f"zi{i}", [128, D], BF16) for i in range(2)]
            SM = [sbt(st, f"sm{i}", [128, 2, 2, 128], F32) for i in range(2)]
            RR = [sbt(st, f"rr{i}", [128, 2, 16, 128], BF16) for i in range(2)]
            DEC = [sbt(st, f"dec{i}", [128, 2, 16, 128], BF16) for i in range(2)]
            WT = sbt(st, "wt", [128, 2, 16, 128], BF16)
            XDT = sbt(st, "xdt", [128, 2, D], BF16)
            T1 = sbt(st, "t1", [128, D], F32)
            T2 = sbt(st, "t2", [128, D], F32)
            GN = sbt(st, "gn", [128, D], BF16)
            SQ2 = sbt(st, "sq2", [128, D], BF16)
            S2 = sbt(st, "s2", [128, 4], F32)
            YO = [sbt(st, f"yo{i}", [128, 8, 128], BF16) for i in range(2)]
            pG = [pst(st, f"pG{i}", [128, 512], F32) for i in range(2)]
            pY = pst(st, "pY", [128, D], F32)
            pO = pst(st, "pO", [128, D], F32)
            pC = pst(st, "pC", [128, D], F32)
            bAC, bHF, bHB, bHBb = Buf(), Buf(), Buf(), Buf()
            bHSs, bHSl = [Buf(), Buf()], [Buf(), Buf()]
            bHSd = Buf()
            bXI, bBI, bXW, bBTc, bCTc, bZI, bYO = ([Buf(), Buf()] for _ in range(7))
            bWT, bXDT, bT1, bT2, bGN, bS2, bSQ2 = (Buf() for _ in range(7))
            bSM, bRR, bDEC = ([Buf(), Buf()] for _ in range(3))
            bRRa = [[Buf() for _ in range(16)] for _ in range(2)]
            bpG = [Buf(), Buf()]
            bpY, bpO, bpC = Buf(), Buf(), Buf()
            bYT = Buf()

            tch = 32
            gi = 0
            for d in range(2):
                for t0 in range(0, NTT, tch):
                    tn = min(tch, NTT - t0)
                    for which, lhs, dst in ((0, TRI2[:, d, :], ACUM), (1, onesf[:], TOTB)):
                        g = gi % 2
                        gi += 1
                        P.pe(lambda e, g=g, lhs=lhs, t0=t0, tn=tn, d=d: e.matmul(
                            pG[g][:, 0:tn * 16].rearrange("p (a b) -> p a b", b=16), lhsT=lhs, rhs=DAS[:, t0:t0 + tn, d * 16:(d + 1) * 16],
                            start=True, stop=True), r=[bK], w=[bpG[g]])
                        P.dve(lambda e, g=g, dst=dst, t0=t0, tn=tn, d=d: e.tensor_copy(
                            out=dst[:, t0:t0 + tn, d * 16:(d + 1) * 16], in_=pG[g][:, 0:tn * 16].rearrange("p (a b) -> p a b", b=16)),
                            r=[bpG[g]], w=[bAC])
            P.dve(lambda e: e.tensor_tensor(out=WDT[:], in0=TOTB[:], in1=ACUM[:], op=ALU.subtract), r=[bAC], w=[bAC])
            P.act(lambda e: e.activation(out=WDT[:], in_=WDT[:], func=AF.Exp), r=[bAC], w=[bAC])
            P.dve(lambda e: e.tensor_tensor(out=WDT[:], in0=WDT[:], in1=DTS[:], op=ALU.mult), r=[bAC], w=[bAC])
            if debug:
                P.dma(DBG[:, 0:NTT * 32], ACUM[:].rearrange("p a b -> p (a b)"), r=[bAC])
            P.act(lambda e: e.activation(out=CD[:], in_=TOTB[:], func=AF.Exp), r=[bAC], w=[bAC])
            P.act(lambda e: e.activation(out=EA[:], in_=ACUM[:], func=AF.Exp), r=[bAC], w=[bAC])
            P.pool(lambda e: e.memset(HF[:], 0.0), w=[bHF])
            P.pool(lambda e: e.memset(HB[:], 0.0), w=[bHB])
            lc = [0]

            def load_xb(gt):
                s = lc[0] % 2
                lc[0] += 1
                P.dma(XI[s][:], XS[gt * 128:(gt + 1) * 128, :], w=[bXI[s]])
                P.dma(BI[s][:], BS[gt * 128:(gt + 1) * 128, :], w=[bBI[s]])
                return s

            def state_update(gt, s, d, H, bH):
                eng = P.dve if d == 0 else P.pool
                eng(lambda e, s=s, gt=gt, d=d: e.tensor_tensor(
                    out=XW[s][:].rearrange("p (h x) -> p h x", h=16), in0=XI[s][:].rearrange("p (h x) -> p h x", h=16),
                    in1=bc(WDT[:, gt, d * 16:(d + 1) * 16], [128, 16, 64], 2), op=ALU.mult), r=[bXI[s], bAC], w=[bXW[s]])
                for g in range(2):
                    P.pe(lambda e, s=s, g=g: e.matmul(pC[:, g * 512:(g + 1) * 512], lhsT=BI[s][:, g * 128:(g + 1) * 128],
                                                      rhs=XW[s][:, g * 512:(g + 1) * 512], start=True, stop=True),
                         r=[bBI[s], bXW[s]], w=[bpC])
                eng(lambda e, gt=gt, d=d, H=H: e.tensor_tensor(
                    out=H[:].rearrange("p (h x) -> p h x", h=16), in0=H[:].rearrange("p (h x) -> p h x", h=16),
                    in1=bc(CD[:, gt, d * 16:(d + 1) * 16], [128, 16, 64], 2), op=ALU.mult), r=[bH, bAC], w=[bH])
                P.dve(lambda e, H=H: e.tensor_tensor(out=H[:], in0=H[:], in1=pC[:], op=ALU.add), r=[bH, bpC], w=[bH])

            seqA = [0, 1] + [2 + lt for lt in range(NTM)]
            for gt in seqA:
                if gt >= 2:
                    lt = gt - 2
                    P.act(lambda e, lt=lt: e.copy(out=HSs[lt % 2][:], in_=HF[:]), r=[bHF], w=[bHSs[lt % 2]])
                    P.dma(HSd[lt], HSs[lt % 2][:], r=[bHSs[lt % 2]], w=[bHSd], q="act")
                    if lt == NTM - 1:
                        break
                s = load_xb(gt)
                state_update(gt, s, 0, HF, bHF)

            seqB = [1, 0] + [2 + lt for lt in range(NTL - 1, -1, -1)]
            pending = []
            mixseq = [gt for gt in seqB if gt >= 2 and gt - 2 < NTM]
            uof = {gt: i % 2 for i, gt in enumerate(mixseq)}

            def mix_head(gt):
                lt = gt - 2
                u = uof[gt]
                P.dve(lambda e, gt=gt: e.tensor_tensor(
                    out=RR[u][:, 0, :, :], in0=bc(DAS[:, gt, 0:16], [128, 16, 128], 2),
                    in1=bc(TRI2[:, 0, :], [128, 16, 128], 1), op=ALU.mult), r=[bK], w=[bRR[u]])
                for hh in range(16):
                    P.act(lambda e, gt=gt, hh=hh: e.activation(out=RR[u][:, 1, hh, :], in_=TRI2[:, 1, :], func=AF.Copy,
                                                               scale=DAS[:, gt, 16 + hh:17 + hh]), r=[bK], w=[bRRa[u][hh]])
                while len(pending) > 0:
                    pending.pop(0)()
                for g in range(2):
                    P.dma(BTc[u][:, g, :], BT[g * 128:(g + 1) * 128, gt * 128:(gt + 1) * 128], w=[bBTc[u]])
                    P.dma(CTc[u][:, g, :], CT[g * 128:(g + 1) * 128, lt * 128:(lt + 1) * 128], w=[bCTc[u]])
                P.dma(ZI[u][:], ZS[lt * 128:(lt + 1) * 128, :], w=[bZI[u]])
                P.dma(HSl[u][:], HSd[lt], w=[bHSl[u]])
                for g in range(2):
                    P.pe(lambda e, u=u, g=g: e.matmul(pG[0][:, g * 128:(g + 1) * 128], lhsT=BTc[u][:, g, :], rhs=CTc[u][:, g, :],
                                                      start=True, stop=True), r=[bBTc[u], bCTc[u]], w=[bpG[0]])
                for d in range(2):
                    P.dve(lambda e, d=d: e.tensor_tensor(out=SM[u][:, d, :, :], in0=pG[0][:, 0:256].rearrange("p (g l) -> p g l", g=2),
                                                         in1=bc(TRI2[:, d, :], [128, 2, 128], 1), op=ALU.mult), r=[bpG[0], bK], w=[bSM[u]])
                k = 1
                for d in range(2):
                    for hq in range(4):
                        g = k % 2
                        k += 1
                        P.pe(lambda e, g=g, d=d, hq=hq: e.matmul(pG[g][:].rearrange("p (h l) -> p h l", h=4), lhsT=UU[:, d, :],
                                                                 rhs=RR[u][:, d, hq * 4:(hq + 1) * 4, :], start=True, stop=True),
                             r=[bRR[u], bK] + (bRRa[u][hq * 4:(hq + 1) * 4] if d == 1 else []), w=[bpG[g]])
                        P.act(lambda e, g=g, d=d, hq=hq: e.activation(out=DEC[u][:, d, hq * 4:(hq + 1) * 4, :],
                                                                      in_=pG[g][:].rearrange("p (h l) -> p h l", h=4), func=AF.Exp),
                              r=[bpG[g]], w=[bDEC[u]])

            def mix_rest(gt, s):
                lt = gt - 2
                u = uof[gt]
                for d in range(2):
                    for g in range(2):
                        eng = P.dve
                        eng(lambda e, d=d, g=g: e.tensor_tensor(out=WT[:, d, g * 8:(g + 1) * 8, :], in0=DEC[u][:, d, g * 8:(g + 1) * 8, :],
                                                                in1=bc(SM[u][:, d, g, :], [128, 8, 128], 1), op=ALU.mult),
                            r=[bDEC[u], bSM[u]], w=[bWT])
                for d in range(2):
                    eng = P.pool
                    eng(lambda e, s=s, gt=gt, d=d: e.tensor_tensor(
                        out=XDT[:, d, :].rearrange("p (h x) -> p h x", h=16), in0=XI[s][:].rearrange("p (h x) -> p h x", h=16),
                        in1=bc(DTS[:, gt, d * 16:(d + 1) * 16], [128, 16, 64], 2), op=ALU.mult), r=[bXI[s], bK], w=[bXDT])
                for h in range(16):
                    for d in range(2):
                        P.pe(lambda e, h=h, d=d: e.matmul(pY[:, h * 64:(h + 1) * 64], lhsT=WT[:, d, h, :], rhs=XDT[:, d, h * 64:(h + 1) * 64],
                                                          start=(d == 0), stop=(d == 1)), r=[bWT, bXDT], w=[bpY])
                P.act(lambda e: e.copy(out=HBb[:], in_=HB[:]), r=[bHB], w=[bHBb])
                for d in range(2):
                    for g in range(2):
                        if d == 0:
                            P.pe(lambda e, u=u, g=g, lt=lt: e.matmul(pO[:, g * 512:(g + 1) * 512], lhsT=CTc[u][:, g, :],
                                                                     rhs=HSl[u][:, g * 512:(g + 1) * 512], start=True, stop=True),
                                 r=[bCTc[u], bHSl[u]], w=[bpO])
                        else:
                            P.pe(lambda e, u=u, g=g: e.matmul(pO[:, g * 512:(g + 1) * 512], lhsT=CTc[u][:, g, :],
                                                              rhs=HBb[:, g * 512:(g + 1) * 512], start=True, stop=True),
                                 r=[bCTc[u], bHBb], w=[bpO])
                    TT, bTT = (T1, bT1) if d == 0 else (T2, bT2)
                    P.dve(lambda e, TT=TT, gt=gt, d=d: e.tensor_tensor(
                        out=TT[:].rearrange("p (h x) -> p h x", h=16), in0=pO[:].rearrange("p (h x) -> p h x", h=16),
                        in1=bc(EA[:, gt, d * 16:(d + 1) * 16], [128, 16, 64], 2), op=ALU.mult), r=[bpO, bAC], w=[bTT])
                P.dve(lambda e: e.tensor_tensor(out=T1[:], in0=T1[:], in1=T2[:], op=ALU.add), r=[bT1, bT2], w=[bT1])
                P.pool(lambda e, s=s: e.tensor_tensor(out=T2[:].rearrange("p (h x) -> p h x", h=16), in0=XI[s][:].rearrange("p (h x) -> p h x", h=16),
                                                      in1=bc(SSDB[:, 64:80], [128, 16, 64], 2), op=ALU.mult), r=[bXI[s], bK, bT2], w=[bT2])
                P.dve(lambda e: e.tensor_tensor(out=T1[:], in0=T1[:], in1=T2[:], op=ALU.add), r=[bT1, bT2], w=[bT1])
                P.dve(lambda e: e.tensor_tensor(out=T1[:], in0=pY[:], in1=T1[:], op=ALU.add), r=[bpY, bT1], w=[bT1])
                P.dve(lambda e, u=u: e.tensor_tensor(out=T1[:], in0=T1[:], in1=ZI[u][:], op=ALU.mult), r=[bT1, bZI[u]], w=[bT1])
                P.act(lambda e: e.activation(out=SQ2[:], in_=T1[:], func=AF.Square, accum_out=S2[:, 0:1]), r=[bT1], w=[bSQ2, bS2])
                P.dve(lambda e: e.tensor_scalar(out=S2[:, 1:2], in0=S2[:, 0:1], scalar1=1.0 / D, scalar2=EPS, op0=ALU.mult, op1=ALU.add), r=[bS2], w=[bS2])
                P.act(lambda e: e.activation(out=S2[:, 2:3], in_=S2[:, 1:2], func=AF.Ln), r=[bS2], w=[bS2])
                P.act(lambda e: e.activation(out=S2[:, 3:4], in_=S2[:, 2:3], func=AF.Exp, scale=-0.5), r=[bS2], w=[bS2])
                P.act(lambda e: e.activation(out=GN[:], in_=T1[:], func=AF.Copy, scale=S2[:, 3:4]), r=[bT1, bS2], w=[bGN])
                def tail(u=u, lt=lt):
                    pTv = pG[1][:].bitcast(BF16).rearrange("p (k t) -> p k t", k=8)
                    for kc in range(8):
                        P.pe(lambda e, kc=kc, pTv=pTv: e.transpose(pTv[:, kc, :], GN[:, kc * 128:(kc + 1) * 128], ident[:]), r=[bGN, bK], w=[bpG[1]])
                    P.dve(lambda e, u=u, pTv=pTv: e.tensor_tensor(out=YO[u][:], in0=pTv, in1=bc(VEC[:, 16:24], [128, 8, 128], 2), op=ALU.mult),
                          r=[bpG[1], bK], w=[bYO[u]])
                    P.dma(YT2[lt, :, 0:8, :], YO[u][:], r=[bYO[u]], w=[bYT], q="pool")
                pending.append(tail)

            for gt in seqB:
                lt = gt - 2
                mix = gt >= 2 and lt < NTM
                s = load_xb(gt)
                if mix:
                    i = mixseq.index(gt)
                    if i == 0:
                        mix_head(gt)
                    if i + 1 < len(mixseq):
                        mix_head(mixseq[i + 1])
                    else:
                        while len(pending) > 0:
                            pending.pop(0)()
                    mix_rest(gt, s)
                if gt != 2:
                    state_update(gt, s, 1, HB, bHB)
            while len(pending) > 0:
                pending.pop(0)()
            P.flush()
        mid2.close()
        if phases <= 2:
            return nc

        with ExitStack() as st:
            KTs = sbt(st, "kts", [128, 4, T], BF16)
            VSs = sbt(st, "vss", [128, NTT, 512], BF16)
            QTs = [sbt(st, f"qts{i}", [128, 512], BF16) for i in range(2)]
            EE = [sbt(st, f"ee{i}", [128, 2, 512], BF16) for i in range(3)]
            R1 = sbt(st, "r1", [128, 512], F32)
            R2 = sbt(st, "r2", [128, 512], F32)
            OA = sbt(st, "oa", [128, 512], F32)
            OB = sbt(st, "ob", [128, 512], F32)
            OO = [sbt(st, f"oo{i}", [128, 512], BF16) for i in range(2)]
            pS = [pst(st, f"pS{i}", [128, 1024], F32) for i in range(2)]
            pO1 = pst(st, "pO1", [128, 512], F32)
            pO2 = pst(st, "pO2", [128, 512], F32)
            LS = sbt(st, "ls", [64, 512], F32)
            pL = pst(st, "pL", [128, 512], F32)
            pB = pst(st, "pB", [128, 512], F32)
            bLS, bpL, bpB = Buf(), Buf(), Buf()
            bKT, bVS = Buf(), Buf()
            bQTs, bOO = [Buf(), Buf()], [Buf(), Buf()]
            bEE = [Buf() for _ in range(3)]
            bpS = [Buf() for _ in range(2)]
            bpO1, bpO2, bpL1, bpL2, bR1, bR2, bOA, bOB, bYT2 = (Buf() for _ in range(9))
            for h in range(4):
                P.dma(KTs[:, h, :], KT[h], w=[bKT])
            vch = 8
            for t0 in range(0, NTT, vch):
                tn = min(vch, NTT - t0)
                P.dma(VSs[:, t0:t0 + tn, :], VS[t0 * 128:(t0 + tn) * 128, :].rearrange("(a p) c -> p a c", p=128), w=[bVS])
            qtiles = []
            q0 = 0
            while q0 < NMIX:
                qw = min(512, NMIX - q0)
                qtiles.append((q0, qw))
                q0 += qw
            steps = [(q0, qw, h, kt) for (q0, qw) in qtiles for h in range(4) for kt in range(NTT)]
            nst = len(steps)
            qslot = {}
            qn = 0
            for (q0, qw) in qtiles:
                for h in range(4):
                    qslot[(q0, h)] = qn % 2
                    qn += 1

            def emit_qk(i):
                q0, qw, h, kt = steps[i]
                qs = qslot[(q0, h)]
                if kt == 0:
                    P.dma(QTs[qs][:, 0:qw], QT[h, :, q0:q0 + qw], w=[bQTs[qs]])
                a = i % 2
                for m in range(2):
                    P.pe(lambda e, a=a, m=m, qs=qs, h=h, kt=kt, qw=qw: e.matmul(
                        pS[a][:, m * 512:m * 512 + qw], lhsT=KTs[m * 64:(m + 1) * 64, h, kt * 128:(kt + 1) * 128],
                        rhs=QTs[qs][m * 64:(m + 1) * 64, 0:qw], start=True, stop=True), r=[bKT, bQTs[qs]], w=[bpS[a]])

            att_def = []
            emit_qk(0)
            for i in range(nst):
                q0, qw, h, kt = steps[i]
                a = i % 2
                x = i % 3
                P.act(lambda e, a=a, x=x, qw=qw: e.activation(out=EE[x][:, :, 0:qw], in_=pS[a][:].rearrange("p (m q) -> p m q", m=2)[:, :, 0:qw],
                                                              func=AF.Exp), r=[bpS[a]], w=[bEE[x]])
                if i + 1 < nst:
                    emit_qk(i + 1)
                for m in range(2):
                    P.pe(lambda e, x=x, m=m, kt=kt, qw=qw: e.matmul(pL[32 * m:32 * m + 32, 0:qw], lhsT=onesb[:, 0:32], rhs=EE[x][:, m, 0:qw],
                                                                     start=(kt == 0), stop=(kt == NTT - 1), tile_position=(0, 32 * m)),
                         r=[bK, bEE[x]], w=[bpL])
                for m, pOx, bpOx in ((0, pO1, bpO1), (1, pO2, bpO2)):
                    P.pe(lambda e, x=x, m=m, pOx=pOx, kt=kt, h=h, qw=qw: e.matmul(pOx[:, 0:qw], lhsT=VSs[:, kt, h * 128:(h + 1) * 128], rhs=EE[x][:, m, 0:qw],
                                                                                 start=(kt == 0), stop=(kt == NTT - 1)), r=[bVS, bEE[x]], w=[bpOx])
                for it in list(att_def):
                    it[0] -= 1
                    if it[0] <= 0:
                        att_def.remove(it)
                        it[1]()
                if kt == NTT - 1:
                    qs = qslot[(q0, h)]
                    P.dve(lambda e, qw=qw: e.tensor_copy(out=OA[:, 0:qw], in_=pO1[:, 0:qw]), r=[bpO1], w=[bOA])
                    P.dve(lambda e, qw=qw: e.tensor_copy(out=OB[:, 0:qw], in_=pO2[:, 0:qw]), r=[bpO2], w=[bOB])
                    P.dve(lambda e, qw=qw: e.tensor_copy(out=LS[:, 0:qw], in_=pL[0:64, 0:qw]), r=[bpL], w=[bLS])

                    def epi(qw=qw, qs=qs, h=h, q0=q0):
                        for m, RR_, bRR_ in ((0, R1, bR1), (1, R2, bR2)):
                            P.pe(lambda e, m=m, qw=qw: e.matmul(pB[:, 0:qw], lhsT=onesf[32 * m:32 * m + 1, :], rhs=LS[32 * m:32 * m + 1, 0:qw], start=True, stop=True),
                                 r=[bLS, bK], w=[bpB])
                            P.dve(lambda e, RR_=RR_, qw=qw: e.reciprocal(out=RR_[:, 0:qw], in_=pB[:, 0:qw]), r=[bpB], w=[bRR_])
                        P.dve(lambda e, qw=qw: e.tensor_tensor(out=OA[:, 0:qw], in0=OA[:, 0:qw], in1=R1[:, 0:qw], op=ALU.mult), r=[bOA, bR1], w=[bOA])
                        P.dve(lambda e, qw=qw: e.tensor_tensor(out=OB[:, 0:qw], in0=OB[:, 0:qw], in1=R2[:, 0:qw], op=ALU.mult), r=[bOB, bR2], w=[bOB])
                        P.dve(lambda e, qw=qw, qs=qs: e.scalar_tensor_tensor(out=OO[qs][:, 0:qw], in0=OB[:, 0:qw], scalar=NLAM[:, 0:1], in1=OA[:, 0:qw],
                                                                             op0=ALU.mult, op1=ALU.add), r=[bOA, bOB, bK], w=[bOO[qs]])
                        P.dma(YT2[q0 // 128:(q0 + qw) // 128, :, 8 + h, :].rearrange("l p t -> p l t"), OO[qs][:, 0:qw].rearrange("p (l t) -> p l t", t=128), r=[bOO[qs]], w=[bYT2], q="pool")
                    att_def.append([4, epi])
            for it in att_def:
                it[1]()
            P.flush()
        if phases <= 3:
            return nc

        stU = top.enter_context(ExitStack())
        WU = sbt(stU, "wu", [128, 8, 2 * DFF], BF16)
        bWU = Buf()
        with ExitStack() as st:
            for kc in range(8):
                P.dma(WU[:, kc, :], w_up[kc * 128:(kc + 1) * 128, :], w=[bWU], q="pool")
            WO = sbt(st, "wo", [128, 12, D], BF16)
            YI = [sbt(st, f"yi{i}", [128, 12, 128], BF16) for i in range(3)]
            XT = [sbt(st, f"xt4{i}", [128, D], F32) for i in range(3)]
            XL = [sbt(st, f"xl{i}", [128, D], F32) for i in range(2)]
            XN = [sbt(st, f"xn4{i}", [128, D], BF16) for i in range(2)]
            SQ = sbt(st, "sq4", [128, D], BF16)
            ST = [sbt(st, f"st4{i}", [128, 4], F32) for i in range(2)]
            HO = [sbt(st, f"ho{i}", [128, 8, 128], BF16) for i in range(2)]
            ZC = sbt(st, "zc", [128, 8, 1], BF16)
            OSQ = sbt(st, "osq", [128, 4, 128], BF16)
            ORS = sbt(st, "ors", [128, 512], F32)
            pN = pst(st, "p4N", [128, 512], F32)
            bOSQ, bORS, bpN = Buf(), Buf(), Buf()
            pM = [pst(st, f"p4M{i}", [128, 512], F32) for i in range(2)]
            pT = [pst(st, f"p4T{i}", [128, 8, 128], BF16) for i in range(2)]
            bWO, bSQ, bZC, bH2, bXL1 = (Buf() for _ in range(5))
            bXL, bXN, bST, bHO, bpM, bpT = ([Buf(), Buf()] for _ in range(6))
            bYI, bXT = [Buf(), Buf(), Buf()], [Buf(), Buf(), Buf()]
            for kc in range(12):
                P.dma(WO[:, kc, :], w_out[kc * 128:(kc + 1) * 128, :], w=[bWO], q="pool")
            P.pool(lambda e: e.memset(ZC[:], 0.0), w=[bZC])
            P.dma(H2T[:, 0:1].rearrange("(k p) t -> p k t", p=128), ZC[:], r=[bZC], w=[bH2], slow=True)
            mi_ = [0]

            def stage1(lt):
                s = lt % 3
                P.dma(YI[s][:], YT2[lt], w=[bYI[s]])
                P.dma(XT[s][:], xall[CTX + lt * 128:CTX + (lt + 1) * 128, :], w=[bXT[s]])
                P.dve(lambda e, s=s: e.tensor_tensor(out=OSQ[:], in0=YI[s][:, 8:12, :], in1=YI[s][:, 8:12, :], op=ALU.mult), r=[bYI[s]], w=[bOSQ])
                P.pe(lambda e: e.matmul(pN[:], lhsT=onesb[:], rhs=OSQ[:].rearrange("p a t -> p (a t)"), start=True, stop=True), r=[bOSQ, bK], w=[bpN])
                P.dve(lambda e: e.tensor_scalar(out=ORS[:], in0=pN[:], scalar1=1.0 / 128, scalar2=EPS, op0=ALU.mult, op1=ALU.add), r=[bpN], w=[bORS])
                P.act(lambda e: e.activation(out=ORS[:], in_=ORS[:], func=AF.Ln), r=[bORS], w=[bORS])
                P.act(lambda e: e.activation(out=ORS[:], in_=ORS[:], func=AF.Exp, scale=-0.5), r=[bORS], w=[bORS])
                P.dve(lambda e, s=s: e.scalar_tensor_tensor(out=YI[s][:, 8:12, :], in0=YI[s][:, 8:12, :], scalar=SGS[:, 0:1],
                                                            in1=ORS[:].rearrange("p (a t) -> p a t", a=4), op0=ALU.mult, op1=ALU.mult),
                      r=[bYI[s], bORS, bK], w=[bYI[s]])

            def stage2(lt):
                s3 = lt % 3
                s = lt % 2
                for cg in range(2):
                    m = mi_[0] % 2
                    mi_[0] += 1
                    for kc in range(12):
                        P.pe(lambda e, m=m, s=s, kc=kc, cg=cg: e.matmul(pM[m][:], lhsT=YI[s3][:, kc, :], rhs=WO[:, kc, cg * 512:(cg + 1) * 512],
                                                                        start=(kc == 0), stop=(kc == 11)), r=[bYI[s3], bWO], w=[bpM[m]])
                    P.dve(lambda e, m=m, s=s, cg=cg: e.tensor_tensor(out=XL[s][:, cg * 512:(cg + 1) * 512], in0=pM[m][:], in1=MOD2B[:, cg * 512:(cg + 1) * 512],
                                                                     op=ALU.mult), r=[bpM[m], bK], w=[bXL[s]])
                P.pool(lambda e, s=s: e.tensor_tensor(out=XL[s][:], in0=XL[s][:], in1=XT[s3][:], op=ALU.add), r=[bXL[s], bXT[s3]], w=[bXL[s]])
                P.dma(XL1[lt * 128:(lt + 1) * 128, :], XL[s][:], r=[bXL[s]], w=[bXL1], q="pool")
                P.act(lambda e, s=s: e.activation(out=SQ[:], in_=XL[s][:], func=AF.Square, accum_out=ST[s][:, 0:1]), r=[bXL[s]], w=[bSQ, bST[s]])
                P.dve(lambda e, s=s: e.tensor_scalar(out=ST[s][:, 1:2], in0=ST[s][:, 0:1], scalar1=1.0 / D, scalar2=EPS, op0=ALU.mult, op1=ALU.add),
                      r=[bST[s]], w=[bST[s]])
                P.act(lambda e, s=s: e.activation(out=ST[s][:, 2:3], in_=ST[s][:, 1:2], func=AF.Ln), r=[bST[s]], w=[bST[s]])
                P.act(lambda e, s=s: e.activation(out=ST[s][:, 3:4], in_=ST[s][:, 2:3], func=AF.Exp, scale=-0.5), r=[bST[s]], w=[bST[s]])
                P.act(lambda e, s=s: e.activation(out=XN[s][:], in_=XL[s][:], func=AF.Copy, scale=ST[s][:, 3:4]), r=[bXL[s], bST[s]], w=[bXN[s]])

            def stageB(lt):
                s = lt % 2
                for kc in range(8):
                    P.pe(lambda e, s=s, kc=kc: e.transpose(pT[s][:, kc, :], XN[s][:, kc * 128:(kc + 1) * 128], ident[:]), r=[bXN[s], bK], w=[bpT[s]])
                for kc in range(8):
                    if True:
                        P.act(lambda e, s=s, kc=kc: e.activation(out=HO[s][:, kc, :], in_=pT[s][:, kc, :], func=AF.Identity,
                                                                 bias=GSH[:, 5, kc:kc + 1], scale=GSH[:, 4, kc:kc + 1]), r=[bpT[s], bK], w=[bHO[s]])
                    else:
                        P.dve(lambda e, s=s, kc=kc: e.tensor_scalar(out=HO[s][:, kc, :], in0=pT[s][:, kc, :], scalar1=GSH[:, 4, kc:kc + 1],
                                                                    scalar2=GSH[:, 5, kc:kc + 1], op0=ALU.mult, op1=ALU.add), r=[bpT[s], bK], w=[bHO[s]])
                P.dma(H2T[:, 1 + lt * 128:1 + (lt + 1) * 128].rearrange("(k p) t -> p k t", p=128), HO[s][:], r=[bHO[s]], w=[bH2], q="act")

            for lt in range(NTM + 2):
                if lt < NTM:
                    stage1(lt)
                if 1 <= lt <= NTM:
                    stage2(lt - 1)
                if lt >= 2:
                    stageB(lt - 2)
            P.flush()

        with ExitStack() as st:
            WD = sbt(st, "wd", [128, 22, D], BF16)
            FW = sbt(st, "fw", [128, 66], F32)
            FB = sbt(st, "fb", [128, 22], F32)
            HW = [sbt(st, "hw0", [128, 8, 512], BF16)] * 2
            C1 = [sbt(st, f"c1{i}", [128, 512], F32) for i in range(2)]
            SG = C1
            GT_ = sbt(st, "gt", [128, 22, 512], BF16)
            XI4 = [sbt(st, f"xi4{i}", [128, D], F32) for i in range(2)]
            X2 = XI4
            SQ = sbt(st, "sq5", [128, D], BF16)
            ST = [sbt(st, f"st5{i}", [128, 4], F32) for i in range(2)]
            OUT = [sbt(st, f"out{i}", [128, D], F32) for i in range(2)]
            pGa = [pst(st, f"pGa{i}", [128, 512], F32) for i in range(2)]
            pVa = [pst(st, f"pVa{i}", [128, 512], F32) for i in range(2)]
            pD = [pst(st, f"pD{i}", [128, 512], F32) for i in range(2)]
            bWD, bFW, bGT, bSQ, bOUTD = (Buf() for _ in range(5))
            bC1, bXI4, bST, bOUT, bpGa, bpVa, bpD = ([Buf(), Buf()] for _ in range(7))
            bHW = [Buf()] * 2
            bSG, bX2 = bC1, bXI4
            for kc in range(22):
                P.dma(WD[:, kc, :], w_down[kc * 128:(kc + 1) * 128, :], w=[bWD], q="pool")
            P.dma(FW[:], fw_pp, w=[bFW])
            P.dma(FB[:], fb_pp, w=[bFW])
            nb4 = (NOWN + 509) // 510
            ti = 0
            di = 0
            for b in range(nb4):
                t0 = b * 510
                nt = min(510, NOWN - t0)
                hs = b % 2
                P.dma(HW[hs][:, :, 0:nt + 2], H2T[:, t0:t0 + nt + 2].rearrange("(k p) t -> p k t", p=128), w=[bHW[hs]])
                for cc in range(22):
                    f = cc % 2
                    for kc in range(8):
                        P.pe(lambda e, f=f, kc=kc, cc=cc, hs=hs, nt=nt: e.matmul(pGa[f][:, 0:nt + 2], lhsT=WU[:, kc, DFF + cc * 128:DFF + (cc + 1) * 128],
                                                                                 rhs=HW[hs][:, kc, 0:nt + 2], start=(kc == 0), stop=(kc == 7)),
                             r=[bWU, bHW[hs]], w=[bpGa[f]])
                    for kc in range(8):
                        P.pe(lambda e, f=f, kc=kc, cc=cc, hs=hs, nt=nt: e.matmul(pVa[f][:, 0:nt], lhsT=WU[:, kc, cc * 128:(cc + 1) * 128],
                                                                                 rhs=HW[hs][:, kc, 1:nt + 1], start=(kc == 0), stop=(kc == 7)),
                             r=[bWU, bHW[hs]], w=[bpVa[f]])
                    P.act(lambda e, f=f, cc=cc, nt=nt: e.activation(out=C1[f][:, 0:nt], in_=pGa[f][:, 1:nt + 1], func=AF.Identity,
                                                                    bias=FB[:, cc:cc + 1], scale=FW[:, cc * 3 + 1:cc * 3 + 2]), r=[bpGa[f], bFW], w=[bC1[f]])
                    for j in (0, 2):
                        P.dve(lambda e, f=f, cc=cc, nt=nt, j=j: e.scalar_tensor_tensor(out=C1[f][:, 0:nt], in0=pGa[f][:, j:j + nt],
                                                                                       scalar=FW[:, cc * 3 + j:cc * 3 + j + 1], in1=C1[f][:, 0:nt],
                                                                                       op0=ALU.mult, op1=ALU.add), r=[bpGa[f], bFW, bC1[f]], w=[bC1[f]])
                    P.act(lambda e, f=f, nt=nt: e.activation(out=SG[f][:, 0:nt], in_=C1[f][:, 0:nt], func=AF.Silu), r=[bC1[f]], w=[bSG[f]])
                    P.dve(lambda e, f=f, cc=cc, nt=nt: e.tensor_tensor(out=GT_[:, cc, 0:nt], in0=SG[f][:, 0:nt], in1=pVa[f][:, 0:nt], op=ALU.mult),
                          r=[bSG[f], bpVa[f]], w=[bGT])
                ts = 0
                while ts < nt:
                    tw = min(128, nt - ts)
                    s = ti % 2
                    ti += 1
                    g0 = t0 + ts
                    P.dma(XI4[s][0:tw, :], XL1[g0:g0 + tw, :], w=[bXI4[s]])
                    for cg in range(2):
                        m = di % 2
                        di += 1
                        for cc in range(22):
                            P.pe(lambda e, m=m, cc=cc, ts=ts, tw=tw, cg=cg: e.matmul(pD[m][0:tw, :], lhsT=GT_[:, cc, ts:ts + tw],
                                                                                     rhs=WD[:, cc, cg * 512:(cg + 1) * 512], start=(cc == 0), stop=(cc == 21)),
                                 r=[bGT, bWD], w=[bpD[m]])
                        P.dve(lambda e, m=m, s=s, tw=tw, cg=cg: e.tensor_tensor(out=OUT[s][0:tw, cg * 512:(cg + 1) * 512], in0=pD[m][0:tw, :],
                                                                                in1=MOD5B[0:tw, cg * 512:(cg + 1) * 512], op=ALU.mult),
                              r=[bpD[m], bK], w=[bOUT[s]])
                    P.pool(lambda e, s=s, tw=tw: e.tensor_tensor(out=X2[s][0:tw, :], in0=OUT[s][0:tw, :], in1=XI4[s][0:tw, :], op=ALU.add),
                           r=[bOUT[s], bXI4[s]], w=[bX2[s]])
                    P.act(lambda e, s=s, tw=tw: e.activation(out=SQ[0:tw, :], in_=X2[s][0:tw, :], func=AF.Square, accum_out=ST[s][0:tw, 0:1]),
                          r=[bX2[s]], w=[bSQ, bST[s]])
                    P.dve(lambda e, s=s, tw=tw: e.tensor_scalar(out=ST[s][0:tw, 1:2], in0=ST[s][0:tw, 0:1], scalar1=1.0 / D, scalar2=EPS,
                                                                op0=ALU.mult, op1=ALU.add), r=[bST[s]], w=[bST[s]])
                    P.act(lambda e, s=s, tw=tw: e.activation(out=ST[s][0:tw, 2:3], in_=ST[s][0:tw, 1:2], func=AF.Sqrt), r=[bST[s]], w=[bST[s]])
                    P.dve(lambda e, s=s, tw=tw: e.reciprocal(out=ST[s][0:tw, 3:4], in_=ST[s][0:tw, 2:3]), r=[bST[s]], w=[bST[s]])
                    P.dve(lambda e, s=s, tw=tw: e.scalar_tensor_tensor(out=OUT[s][0:tw, :], in0=X2[s][0:tw, :], scalar=ST[s][0:tw, 3:4],
                                                                       in1=FGB[0:tw, :], op0=ALU.mult, op1=ALU.mult),
                          r=[bX2[s], bST[s], bK], w=[bOUT[s]])
                    P.dma(out[g0:g0 + tw, :], OUT[s][0:tw, :], r=[bOUT[s]], w=[bOUTD], q="pool")
                    ts += tw
            P.flush(final=True)
    return nc


def rope_table(NL, flip):
    t = np.arange(NL)
    pos = (NL - 1 - t) if flip else t
    row = (pos // 64).astype(np.float32)
    col = (pos % 64).astype(np.float32)
    inv = np.power(np.float32(10000.0), -np.arange(0, 32, 2, dtype=np.float32) / np.float32(32)).astype(np.float32)
    ar = row[:, None] * inv
    ac = col[:, None] * inv
    cos = np.concatenate([np.cos(ar), np.cos(ar), np.cos(ac), np.cos(ac)], axis=1)
    sin = np.concatenate([-np.sin(ar), np.sin(ar), -np.sin(ac), np.sin(ac)], axis=1)
    return np.ascontiguousarray(np.concatenate([cos, sin], axis=1).astype(np.float32))


def pp(v, n):
    return np.ascontiguousarray(np.asarray(v, np.float32).reshape(n, 128).T)


def core_inputs(inp, b, half, NL):
    flip = half == 1
    f32 = lambda a: np.ascontiguousarray(np.asarray(a, np.float32))
    x = np.asarray(inp["x"])[b]
    cx = np.asarray(inp["ctx"])[b]
    if flip:
        x = x[::-1]
        cx = cx[::-1]
    xall = f32(np.concatenate([cx, x], axis=0))
    w_in = np.asarray(inp["w_in"])[0]
    order = [1, 0] if flip else [0, 1]
    if flip:
        w_in = np.concatenate([w_in[:, :DT0], w_in[:, DT0 + 16:DT0 + 32], w_in[:, DT0:DT0 + 16], w_in[:, DT0 + 32:]], axis=1)
    cw = np.asarray(inp["conv_w"])[0]
    fw = np.asarray(inp["ffn_conv_w"])[0]
    if flip:
        cw = cw[::-1]
        fw = fw[::-1]
    cw_pp = np.stack([pp(cw[j], 12) for j in range(3)], axis=2).reshape(128, 36)
    fw_pp = np.stack([pp(fw[j], 22) for j in range(3)], axis=2).reshape(128, 66)
    ssd_row = np.concatenate([np.asarray(inp[k])[0][order].reshape(-1) for k in ("a_log", "dt_bias", "d_skip")])[None, :]
    lam_row = np.concatenate([np.asarray(inp[k])[0] for k in ("lam_q1", "lam_k1", "lam_q2", "lam_k2")])[None, :]
    vec = np.zeros((128, 32), np.float32)
    vec[:, 0:8] = pp(np.asarray(inp["norm1_g"])[0], 8)
    vec[:, 8:16] = pp(np.asarray(inp["norm2_g"])[0], 8)
    vec[:, 16:24] = pp(np.asarray(inp["ssd_norm_g"])[0], 8)
    vec[:, 24] = np.asarray(inp["subln_g"])[0]
    c_pp = np.concatenate([pp(np.asarray(inp["c"])[b], 8), pp(np.asarray(inp["c_ctx"]), 8)], axis=1)
    return {
        "xall": xall, "c_pp": f32(c_pp), "w_mod": f32(np.asarray(inp["w_mod"])[0]), "b_mod": f32(np.asarray(inp["b_mod"])[0][None, :]),
        "vec_pp": vec, "w_in": f32(w_in), "cw_pp": f32(cw_pp), "cb_pp": pp(np.asarray(inp["conv_b"])[0], 12),
        "cb_row": f32(np.asarray(inp["conv_b"])[0][None, :]), "ssd_row": f32(ssd_row), "lam_row": f32(lam_row),
        "w_out": f32(np.asarray(inp["w_out"])[0]), "w_up": f32(np.asarray(inp["w_up"])[0]), "fw_pp": f32(fw_pp),
        "fb_pp": pp(np.asarray(inp["ffn_conv_b"])[0], 22), "w_down": f32(np.asarray(inp["w_down"])[0]),
        "fg_row": f32(np.asarray(inp["final_g"])[None, :]), "rope": rope_table(NL, flip),
    }


_NC_CACHE = {}


def kernel(**inputs):
    NL = int(np.asarray(inputs["x"]).shape[1])
    B = int(np.asarray(inputs["x"]).shape[0])
    if NL not in _NC_CACHE:
        _NC_CACHE[NL] = build(NL)
    nc = _NC_CACHE[NL]
    in_maps = [core_inputs(inputs, c // 2, c % 2, NL) for c in range(2 * B)]
    res = run_bass_kernel_spmd(nc, in_maps, core_ids=list(range(2 * B)))
    out = np.empty((B, NL, D), np.float32)
    h = NL // 2
    for c in range(2 * B):
        o = np.asarray(res.results[c]["out"], np.float32)
        if c % 2 == 0:
            out[c // 2, :h] = o
        else:
            out[c // 2, h:] = o[::-1]
    return out
```
